# Optimizing a Trainium2 kernel written in Bass

```python
import math
import jax, jax.numpy as jnp
from jax import lax
import numpy as np

D_MODEL = 2048
BATCH = 2
SEQ = 16384
DEPTH = 2

GRID_W = 64
CTX_LEN = 256

POOL_WIDTH = D_MODEL // 4
CONV_WIDTH = D_MODEL // 4
NA_WIDTH = D_MODEL - POOL_WIDTH - CONV_WIDTH
MIX_WIDTH = POOL_WIDTH + NA_WIDTH + CONV_WIDTH
POOL_WINDOWS = (2, 4, 8, 16)
POOL_GROUP = POOL_WIDTH // len(POOL_WINDOWS)
NA_HEAD_DIM = 64
NA_HEADS = NA_WIDTH // NA_HEAD_DIM
NA_KH = 8
NA_KW = 16
ROPE_THETA = 10000.0
CONV_K = 31
NORM_EPS = 1e-6
LN_EPS = 1e-5

OFF_A_V = 0
OFF_A_G = OFF_A_V + POOL_WIDTH
OFF_B_Q = OFF_A_G + POOL_WIDTH
OFF_B_K = OFF_B_Q + NA_WIDTH
OFF_B_V = OFF_B_K + NA_WIDTH
OFF_B_G = OFF_B_V + NA_WIDTH
OFF_C_U = OFF_B_G + NA_WIDTH
OFF_C_G = OFF_C_U + 2 * CONV_WIDTH
PROJ_WIDTH = OFF_C_G + CONV_WIDTH

kernel_name = "hybrid_pool_natten_conformer_dit"


def rms_norm(x, g):
    xf = x.astype(jnp.float32)
    y = xf * lax.rsqrt(jnp.mean(xf * xf, axis=-1, keepdims=True) + NORM_EPS)
    return (y * g.astype(jnp.float32)).astype(x.dtype)


def to_heads(t):
    return t.reshape(t.shape[:-1] + (NA_HEADS, NA_HEAD_DIM))


def axial_rope(x, rows_pos, cols_pos):
    half = x.shape[-1] // 2
    freqs = ROPE_THETA ** (-jnp.arange(0, half, 2, dtype=jnp.float32) / half)

    def rot(xp, pos):
        ang = pos.astype(jnp.float32)[:, None] * freqs[None, :]
        cos = jnp.cos(ang)[None, :, None, :]
        sin = jnp.sin(ang)[None, :, None, :]
        xf = xp.astype(jnp.float32)
        x1, x2 = xf[..., : half // 2], xf[..., half // 2:]
        return jnp.concatenate([x1 * cos - x2 * sin, x1 * sin + x2 * cos], axis=-1)

    out = jnp.concatenate([rot(x[..., :half], rows_pos), rot(x[..., half:], cols_pos)], axis=-1)
    return out.astype(x.dtype)


def multiscale_pool(u, w_pool, pool_scale):
    B, L, _ = u.shape
    uf = u.astype(jnp.float32)
    cs = jnp.concatenate([jnp.zeros((B, 1, POOL_WIDTH), jnp.float32), jnp.cumsum(uf, axis=1)], axis=1)
    t = jnp.arange(L)
    means = []
    for g, w in enumerate(POOL_WINDOWS):
        lo = jnp.clip(t - w // 2, 0, L - 1)
        hi = jnp.clip(t + (w - w // 2) - 1, 0, L - 1)
        seg = cs[..., g * POOL_GROUP:(g + 1) * POOL_GROUP]
        total = jnp.take(seg, hi + 1, axis=1) - jnp.take(seg, lo, axis=1)
        means.append(total / (hi - lo + 1).astype(jnp.float32)[None, :, None])
    pooled = (jnp.concatenate(means, axis=-1) - uf).astype(u.dtype)
    y = jnp.einsum('blgc,gcd->blgd', pooled.reshape(B, L, len(POOL_WINDOWS), POOL_GROUP), w_pool)
    return y.reshape(B, L, POOL_WIDTH) * pool_scale


def neighbourhood_attention(q, k, v, kc, vc, rpb):
    B, L, H, hd = q.shape
    rows = L // GRID_W
    kh = min(NA_KH, rows)
    scale = hd ** -0.5
    qg = jnp.moveaxis((q * scale).reshape(B, rows, GRID_W, H, hd), 1, 0)
    kg = k.reshape(B, rows, GRID_W, H, hd)
    vg = v.reshape(B, rows, GRID_W, H, hd)
    kc_s = kc
    col = np.arange(GRID_W)
    col_start = np.clip(col - NA_KW // 2, 0, GRID_W - NA_KW)
    col_idx = col_start[:, None] + np.arange(NA_KW)[None, :]
    dc_idx = col_idx - col[:, None] + NA_KW - 1
    n_loc = kh * NA_KW

    def one_row(args):
        r, q_r = args
        rs = jnp.clip(r - kh // 2, 0, rows - kh)
        k_rows = lax.dynamic_slice_in_dim(kg, rs, kh, axis=1)
        v_rows = lax.dynamic_slice_in_dim(vg, rs, kh, axis=1)
        k_win = k_rows[:, :, col_idx]
        v_win = v_rows[:, :, col_idx]
        dr_idx = rs + jnp.arange(kh) - r + NA_KH - 1
        bias = rpb[:, dr_idx[None, :, None], dc_idx[:, None, :]]
        s_loc = jnp.einsum('bwhd,bawkhd->bhwak', q_r, k_win).astype(jnp.float32) + bias[None].astype(jnp.float32)
        s_ctx = jnp.einsum('bwhd,bchd->bhwc', q_r, kc_s).astype(jnp.float32)
        s = jnp.concatenate([s_loc.reshape(B, H, GRID_W, n_loc), s_ctx], axis=-1)
        p = jax.nn.softmax(s, axis=-1).astype(v.dtype)
        p_loc = p[..., :n_loc].reshape(B, H, GRID_W, kh, NA_KW)
        p_ctx = p[..., n_loc:]
        return (jnp.einsum('bhwak,bawkhd->bwhd', p_loc, v_win)
                + jnp.einsum('bhwc,bchd->bwhd', p_ctx, vc))

    out = lax.map(one_row, (jnp.arange(rows), qg))
    return jnp.moveaxis(out, 0, 1).reshape(B, L, H * hd)


def context_attention(qc, kc, vc):
    B, Lc, H, hd = qc.shape
    s = jnp.einsum('bqhd,bchd->bhqc', qc * (hd ** -0.5), kc).astype(jnp.float32)
    p = jax.nn.softmax(s, axis=-1).astype(vc.dtype)
    return jnp.einsum('bhqc,bchd->bqhd', p, vc).reshape(B, Lc, H * hd)


def conformer_conv(u, conv_dw, conv_dw_b, ln_g, ln_b, conv_pw, conv_pw_b):
    y = u[..., :CONV_WIDTH] * jax.nn.sigmoid(u[..., CONV_WIDTH:])
    y = lax.conv_general_dilated(
        y, conv_dw[:, None, :], window_strides=(1,),
        padding=((CONV_K // 2, CONV_K // 2),),
        dimension_numbers=('NWC', 'WIO', 'NWC'),
        feature_group_count=CONV_WIDTH) + conv_dw_b
    yf = y.astype(jnp.float32)
    mu = jnp.mean(yf, axis=-1, keepdims=True)
    var = jnp.mean(jnp.square(yf - mu), axis=-1, keepdims=True)
    yn = ((yf - mu) * lax.rsqrt(var + LN_EPS) * ln_g.astype(jnp.float32) + ln_b.astype(jnp.float32)).astype(y.dtype)
    return jax.nn.silu(yn) @ conv_pw + conv_pw_b


def mix_out(p, na_out, w_pool, pool_scale, conv_dw, conv_dw_b, conv_ln_g, conv_ln_b, conv_pw, conv_pw_b, w_out):
    ya = multiscale_pool(p[..., OFF_A_V:OFF_A_G], w_pool, pool_scale) * jax.nn.silu(p[..., OFF_A_G:OFF_B_Q])
    yb = na_out * jax.nn.silu(p[..., OFF_B_G:OFF_C_U])
    yc = conformer_conv(p[..., OFF_C_U:OFF_C_G], conv_dw, conv_dw_b, conv_ln_g, conv_ln_b,
                        conv_pw, conv_pw_b) * jax.nn.silu(p[..., OFF_C_G:PROJ_WIDTH])
    return jnp.concatenate([ya, yb, yc], axis=-1) @ w_out


def hybrid_layer(x, xc, c_act, cc_act, w_mod, b_mod, norm_g, w_in, w_pool, pool_scale, na_rpb,
                 conv_dw, conv_dw_b, conv_ln_g, conv_ln_b, conv_pw, conv_pw_b, w_out,
                 rows_pos, cols_pos, update_ctx):
    shift, scale, gate = jnp.split(c_act @ w_mod + b_mod, 3, axis=-1)
    shift_c, scale_c, gate_c = jnp.split(cc_act @ w_mod + b_mod, 3, axis=-1)

    hc = rms_norm(xc, norm_g) * (1 + scale_c) + shift_c
    if update_ctx:
        pc = hc @ w_in
        kv_c = pc[..., OFF_B_K:OFF_B_G]
    else:
        kv_c = hc @ w_in[:, OFF_B_K:OFF_B_G]
    kc = to_heads(kv_c[..., :NA_WIDTH])
    vc = to_heads(kv_c[..., NA_WIDTH:])

    h = rms_norm(x, norm_g) * (1 + scale[:, None, :]) + shift[:, None, :]
    p = h @ w_in
    q = axial_rope(to_heads(p[..., OFF_B_Q:OFF_B_K]), rows_pos, cols_pos)
    k = axial_rope(to_heads(p[..., OFF_B_K:OFF_B_V]), rows_pos, cols_pos)
    v = to_heads(p[..., OFF_B_V:OFF_B_G])
    na_lat = neighbourhood_attention(q, k, v, kc, vc, na_rpb)
    x_new = x + gate[:, None, :] * mix_out(p, na_lat, w_pool, pool_scale, conv_dw, conv_dw_b,
                                           conv_ln_g, conv_ln_b, conv_pw, conv_pw_b, w_out)
    if update_ctx:
        qc = to_heads(pc[..., OFF_B_Q:OFF_B_K])
        na_c = context_attention(qc, kc, vc)
        xc = xc + gate_c * mix_out(pc, na_c, w_pool, pool_scale, conv_dw, conv_dw_b,
                                   conv_ln_g, conv_ln_b, conv_pw, conv_pw_b, w_out)
    return x_new, xc


def setup_inputs(seed: int = 0) -> dict:
    key = jax.random.key(seed)
    ks = jax.random.split(key, 24)
    f32 = jnp.float32
    nrm = lambda k, shape, s: jax.random.normal(k, shape, f32) * s
    return {
        "x": nrm(ks[0], (BATCH, SEQ, D_MODEL), 1.0),
        "c": nrm(ks[1], (BATCH, D_MODEL), 1.0),
        "ctx": nrm(ks[2], (BATCH, CTX_LEN, D_MODEL), 1.0),
        "c_ctx": nrm(ks[3], (D_MODEL,), 1.0),
        "w_mod": nrm(ks[4], (DEPTH, D_MODEL, 3 * D_MODEL), 0.5 * D_MODEL ** -0.5),
        "b_mod": nrm(ks[5], (DEPTH, 3 * D_MODEL), 0.01),
        "norm_g": 1.0 + nrm(ks[6], (DEPTH, D_MODEL), 0.02),
        "w_in": nrm(ks[7], (DEPTH, D_MODEL, PROJ_WIDTH), D_MODEL ** -0.5),
        "w_pool": nrm(ks[8], (DEPTH, len(POOL_WINDOWS), POOL_GROUP, POOL_GROUP), POOL_GROUP ** -0.5),
        "pool_scale": 1.0 + nrm(ks[9], (DEPTH, POOL_WIDTH), 0.1),
        "na_rpb": nrm(ks[10], (DEPTH, NA_HEADS, 2 * NA_KH - 1, 2 * NA_KW - 1), 0.1),
        "conv_dw": nrm(ks[11], (DEPTH, CONV_K, CONV_WIDTH), CONV_K ** -0.5),
        "conv_dw_b": nrm(ks[12], (DEPTH, CONV_WIDTH), 0.01),
        "conv_ln_g": 1.0 + nrm(ks[13], (DEPTH, CONV_WIDTH), 0.02),
        "conv_ln_b": nrm(ks[14], (DEPTH, CONV_WIDTH), 0.01),
        "conv_pw": nrm(ks[15], (DEPTH, CONV_WIDTH, CONV_WIDTH), CONV_WIDTH ** -0.5),
        "conv_pw_b": nrm(ks[16], (DEPTH, CONV_WIDTH), 0.01),
        "w_out": nrm(ks[17], (DEPTH, MIX_WIDTH, D_MODEL), MIX_WIDTH ** -0.5),
        "final_norm_g": 1.0 + nrm(ks[18], (D_MODEL,), 0.02),
    }


def reference(x, c, ctx, c_ctx, w_mod, b_mod, norm_g, w_in, w_pool, pool_scale, na_rpb,
              conv_dw, conv_dw_b, conv_ln_g, conv_ln_b, conv_pw, conv_pw_b, w_out, final_norm_g):
    L = x.shape[1]
    t = jnp.arange(L)
    rows_pos = t // GRID_W
    cols_pos = t % GRID_W
    c_act = jax.nn.silu(c)
    cc_act = jax.nn.silu(c_ctx)
    xc = ctx
    for i in range(DEPTH):
        x, xc = hybrid_layer(
            x, xc, c_act, cc_act, w_mod[i], b_mod[i], norm_g[i], w_in[i], w_pool[i], pool_scale[i],
            na_rpb[i], conv_dw[i], conv_dw_b[i], conv_ln_g[i], conv_ln_b[i], conv_pw[i], conv_pw_b[i],
            w_out[i], rows_pos, cols_pos, i < DEPTH - 1)
    return rms_norm(x, final_norm_g)
```

```python
import os
import numpy as np
import concourse.bass as bass
import concourse.mybir as mybir
from concourse.bass_utils import run_bass_kernel_spmd

F32 = mybir.dt.float32
BF16 = mybir.dt.bfloat16
AF = mybir.ActivationFunctionType
ALU = mybir.AluOpType

D = 2048
PW = 6656
NCH = 52
H = 16
CTX = 256
GW = 64
NEG = -30000.0
NSLOT = 6
CLS_SLOTS = {0: [-2, -1, 0, 1, 2], 1: [-2, -1, 0, 1, 2, 3], 2: [-2, -1, 0, 1, 2],
             3: [-2, -1, 0, 1, 2], 4: [-3, -2, -1, 0, 1, 2]}


class Res:
    __slots__ = ("name", "w", "rs")

    def __init__(self, name=""):
        self.name = name
        self.w = None
        self.rs = []


class Op:
    __slots__ = ("eng", "fn", "deps", "sig", "need_sig", "chan", "ndma", "bar")


class Prog:
    ENGS = ("pe", "act", "dve", "pool", "sp")

    def __init__(self, nc):
        self.nc = nc
        self.streams = {e: [] for e in self.ENGS}
        self.ops = []
        self.last_eng = {}
        self.last_chan = {}

    def op(self, eng, fn, reads=(), writes=(), chan=None, ndma=1, extra=()):
        o = Op()
        o.eng = eng
        o.fn = fn
        o.chan = chan
        o.ndma = ndma
        o.need_sig = chan is not None
        o.bar = False
        o.sig = None
        deps = {}
        for r in reads:
            if r.w is not None:
                deps[id(r.w)] = r.w
        for w in writes:
            if w.w is not None:
                deps[id(w.w)] = w.w
            for x in w.rs:
                deps[id(x)] = x
        for x in extra:
            deps[id(x)] = x
        dl = []
        for d in deps.values():
            if d is o:
                continue
            if eng == "pe" and d.eng == "pe" and d.chan is None and chan is None:
                continue
            d.need_sig = True
            dl.append(d)
        o.deps = dl
        for r in reads:
            r.rs.append(o)
        for w in writes:
            w.w = o
            w.rs = []
        self.streams[eng].append(o)
        self.ops.append(o)
        if chan is None:
            self.last_eng[eng] = o
        else:
            self.last_chan[chan] = o
        return o

    def barrier(self):
        deps = list(self.last_eng.values()) + list(self.last_chan.values())
        keep = dict(self.last_eng)
        for eng in self.ENGS:
            o = self.op(eng, lambda e: e.nop(), extra=deps)
        self.last_eng = keep
        o.bar = True
        self.last_chan = {}

    def emit(self):
        nc = self.nc
        from contextlib import ExitStack
        es = ExitStack()
        engsem = {e: es.enter_context(nc.semaphore("s_" + e)) for e in self.ENGS}
        chansem = {}
        cnt = {e: 0 for e in self.ENGS}
        ccnt = {}
        free = {True: [], False: []}
        nslots = 0
        for o in self.ops:
            if o.bar:
                for slot in chansem.values():
                    free[slot[2]].append(slot)
                chansem = {}
            if not o.need_sig:
                continue
            if o.chan is None:
                cnt[o.eng] += 1
                o.sig = (engsem[o.eng], cnt[o.eng])
            else:
                if o.chan not in chansem:
                    sw = o.eng == "pool"
                    if free[sw]:
                        chansem[o.chan] = free[sw].pop()
                    else:
                        nslots += 1
                        chansem[o.chan] = [es.enter_context(nc.semaphore("c%d" % nslots)), 0, sw]
                slot = chansem[o.chan]
                slot[1] += 16 * o.ndma
                o.sig = (slot[0], slot[1])
        self.nsem = nslots + len(engsem)
        streams = self.streams

        def run(ename, e):
            known = {}
            for o in streams[ename]:
                need = {}
                for d in o.deps:
                    s, v = d.sig
                    k = id(s)
                    if k not in need or need[k][1] < v:
                        need[k] = (s, v)
                for k, (s, v) in need.items():
                    if known.get(k, 0) < v:
                        e.wait_ge(s, v)
                        known[k] = v
                ins = o.fn(e)
                if o.need_sig:
                    if o.chan is None:
                        ins.then_inc(o.sig[0], 1)
                    else:
                        if not isinstance(ins, (list, tuple)):
                            ins = [ins]
                        assert len(ins) == o.ndma, (len(ins), o.ndma)
                        for i_ in ins:
                            i_.then_inc(o.sig[0], 16)

        with nc.Block() as block:
            @block.tensor
            def _(e):
                run("pe", e)

            @block.scalar
            def _(e):
                run("act", e)

            @block.vector
            def _(e):
                run("dve", e)

            @block.gpsimd
            def _(e):
                run("pool", e)

            @block.sync
            def _(e):
                run("sp", e)
        es.close()


class Tl:
    _n = [0]

    def __init__(self, ap, name):
        self.ap = ap
        self.r = Res(name)
        Tl._n[0] += 1
        self.chan = "%s_%d" % (name, Tl._n[0])


class Rot:
    def __init__(self, tiles):
        self.t = tiles
        self.i = 0

    def next(self):
        t = self.t[self.i % len(self.t)]
        self.i += 1
        return t


class Arena:
    def __init__(self, nc, nbytes):
        self.t = nc.alloc_sbuf_tensor("arena", [128, nbytes // 2], BF16)
        self.nbytes = nbytes
        self.top = 0

    def alloc(self, shape, dt, name="t"):
        n = 1
        for s in shape:
            n *= s
        nb = n * (4 if dt == F32 else 2)
        off = (self.top + 63) // 64 * 64
        self.top = off + nb
        assert self.top <= self.nbytes, ("SBUF arena overflow", name, self.top)
        v = self.t[:, off // 2:(off + nb) // 2]
        if dt == F32:
            v = v.bitcast(F32)
        if len(shape) == 2:
            v = v.rearrange("p (a b) -> p a b", a=shape[0])
        elif len(shape) == 3:
            v = v.rearrange("p (a b c) -> p a b c", a=shape[0], b=shape[1])
        return Tl(v, name)


class Scr:
    def __init__(self, ap, ntok, name):
        self.ap = ap
        self.rs = [Res(name) for _ in range((ntok + 127) // 128)]

    def res(self, t0, t1):
        return self.rs[t0 // 128:(t1 + 127) // 128]


class _Stop(Exception):
    pass


def build(OWN, stop=99):
    def chk(k):
        if stop == k:
            raise _Stop()
    E = OWN + 16
    NTOK = E * GW
    NP = E // 2
    nc = bass.Bass("TRN2", target_bir_lowering=False)
    P = Prog(nc)

    def din(name, shape, dt=F32):
        return nc.dram_tensor(name, list(shape), dt, kind="ExternalInput").ap()

    def dscr(name, shape, dt=BF16):
        return nc.dram_tensor(name, list(shape), dt, kind="Internal").ap()

    xloc = din("xloc", [NTOK, D])
    ctxin = din("ctxin", [CTX, D])
    cvec = din("cvec", [128, 32])
    w_mod = din("w_mod", [2, D, 3 * D])
    bmod = din("bmod", [128, 2 * 48])
    normg = din("normg", [128, 2 * 16])
    w_in = din("w_in", [2, D, PW])
    w_pool = din("w_pool", [2, 4, 128, 128])
    pscale = din("pscale", [128, 8])
    cdw = din("cdw", [128, 2 * 4 * 31])
    cvecs = din("cvecs", [128, 2 * 4 * 4])
    conv_pw = din("conv_pw", [2, 512, 512])
    w_out = din("w_out", [2, D, D])
    fng = din("fng", [1, D])
    cosT = din("cosT", [128, NTOK])
    sinT = din("sinT", [128, NTOK])
    maskT = din("maskT", [128, NTOK])
    cmask = din("cmask", [128, CTX])
    ebraw = din("ebraw", [2, 5, 128, H * NSLOT * 128])
    permin = din("permin", [128, 128])
    out = nc.dram_tensor("out", [OWN * GW, D], F32, kind="ExternalOutput").ap()

    wbf = [dscr("wbf%d" % l, [D, PW]) for l in range(2)]
    woutbf = [dscr("woutbf%d" % l, [16, 128, 2048]) for l in range(2)]
    ebs = [dscr("ebs%d" % l, [5, 128, H * NSLOT * 128]) for l in range(2)]
    modrow = dscr("modrow", [4, D], F32)
    wres = {k: Res(k) for k in ("wbf0", "wbf1", "wout0", "wout1", "ebs0", "ebs1", "modrow")}

    class Seg:
        pass

    def mkseg(name, ntok, xsrc, rope, midx):
        s = Seg()
        s.name = name
        s.ntok = ntok
        s.rope = rope
        s.midx = midx
        s.x = [Scr(xsrc, ntok, name + "x0"), Scr(dscr(name + "_x1", [ntok, D], F32), ntok, name + "x1")]
        s.qT = Scr(dscr(name + "_qT", [8, 128, ntok]), ntok, name + "qT")
        s.kT = Scr(dscr(name + "_kT", [8, 128, ntok]), ntok, name + "kT")
        s.v = Scr(dscr(name + "_v", [ntok, H * 65]), ntok, name + "v")
        s.gate = Scr(dscr(name + "_g", [16, 128, ntok]), ntok, name + "g")
        s.uT = Scr(dscr(name + "_uT", [4, 128, ntok]), ntok, name + "uT")
        s.yT = Scr(dscr(name + "_yT", [4, 128, ntok]), ntok, name + "yT")
        return s

    lat = mkseg("lat", NTOK, xloc, True, 0)
    cseg = mkseg("ctx", CTX, ctxin, False, 1)
    lat.mask = maskT
    cseg.mask = cmask
    outres = Scr(out, OWN * GW, "out")

    ar = Arena(nc, 207 * 1024)
    A = ar.alloc

    identf = A([128], F32, "identf")
    ident = A([128], BF16, "ident")
    perm = A([128], BF16, "perm")
    onesm = A([128], F32, "onesm")
    eps6 = A([1], F32, "eps6")
    eps5 = A([1], F32, "eps5")
    cact = A([16, 2], F32, "cact")
    bmod_s = A([2, 48], F32, "bmod")
    normg_s = A([2, 16], F32, "normg")
    pscale_s = A([2, 4], F32, "pscale")
    cdw_s = A([2, 4, 31], F32, "cdw")
    cvecs_s = A([2, 4, 4], F32, "cvecs")
    modv = A([2, 48, 2], F32, "modv")
    Amod = A([2, 2, 16], F32, "Amod")
    wpool_s = A([4, 128], BF16, "wpool")
    cpw_s = A([4, 512], BF16, "cpw")
    ctxK = A([8, CTX], BF16, "ctxK")
    ctxV = A([2, H, 65], BF16, "ctxV")
    persist_top = ar.top

    psA = [Tl(nc.alloc_psum_tensor("psA%d" % i, [128, 512], F32)[:], "psA") for i in range(3)]
    psB = [Tl(nc.alloc_psum_tensor("psB%d" % i, [128, 512], F32)[:], "psB") for i in range(1)]
    psS = [Tl(nc.alloc_psum_tensor("psS%d" % i, [128, 1024], F32)[:], "psS") for i in range(2)]
    psrot = Rot(psA)

    def dma_in(dst, dst_ap, src_ap, src_res, eng="sp"):
        return P.op(eng, lambda e: e.dma_start(out=dst_ap, in_=src_ap), reads=src_res, writes=[dst.r], chan=dst.chan)

    def dma_out(src, src_ap, dst_ap, dst_res, eng="pool"):
        return P.op(eng, lambda e: e.dma_start(out=dst_ap, in_=src_ap), reads=[src.r], writes=dst_res, chan=src.chan + "o")

    P.op("pool", lambda e: e.memset(identf.ap, 1.0), writes=[identf.r])
    P.op("pool", lambda e: e.affine_select(out=identf.ap, in_=identf.ap, pattern=[[-1, 128]], compare_op=ALU.is_equal,
                                          fill=0.0, base=0, channel_multiplier=1), reads=[identf.r], writes=[identf.r])
    P.op("dve", lambda e: e.tensor_copy(out=ident.ap, in_=identf.ap), reads=[identf.r], writes=[ident.r])
    P.op("pool", lambda e: e.memset(onesm.ap, 1.0 / 512), writes=[onesm.r])
    P.op("pool", lambda e: e.memset(eps6.ap, 1e-6), writes=[eps6.r])
    P.op("pool", lambda e: e.memset(eps5.ap, 1e-5), writes=[eps5.r])

    ph0 = ar.top
    permf = A([128], F32, "permf")
    cvec_s = A([16, 2], F32, "cvec")
    dma_in(permf, permf.ap, permin, [])
    P.op("dve", lambda e: e.tensor_copy(out=perm.ap, in_=permf.ap), reads=[permf.r], writes=[perm.r])
    dma_in(cvec_s, cvec_s.ap, cvec.rearrange("p (a b) -> p a b", a=16), [])
    dma_in(bmod_s, bmod_s.ap, bmod.rearrange("p (a b) -> p a b", a=2), [])
    dma_in(normg_s, normg_s.ap, normg.rearrange("p (a b) -> p a b", a=2), [])
    dma_in(pscale_s, pscale_s.ap, pscale.rearrange("p (a b) -> p a b", a=2), [])
    dma_in(cdw_s, cdw_s.ap, cdw.rearrange("p (a b c) -> p a b c", a=2, b=4), [])
    dma_in(cvecs_s, cvecs_s.ap, cvecs.rearrange("p (a b c) -> p a b c", a=2, b=4), [])
    P.op("act", lambda e: e.activation(out=cact.ap, in_=cvec_s.ap, func=AF.Silu), reads=[cvec_s.r], writes=[cact.r])

    wm = Rot([A([3 * D], F32, "wm") for _ in range(2)])
    psM = psA[0]
    def mod_gen():
        for l in range(2):
            for kc in range(16):
                w = wm.next()
                dma_in(w, w.ap, w_mod[l, kc * 128:(kc + 1) * 128, :], [])

                def mm(e, w=w, kc=kc):
                    for fo in range(48):
                        i = e.matmul(psM.ap[:, 2 * fo:2 * fo + 2], lhsT=w.ap[:, fo * 128:(fo + 1) * 128], rhs=cact.ap[:, kc, :],
                                     start=True, stop=True)
                    return i
                P.op("pe", mm, reads=[w.r, cact.r], writes=[psM.r])
                mv = modv.ap[:, l].rearrange("p a b -> p (a b)")
                if kc == 0:
                    P.op("dve", lambda e, mv=mv: e.tensor_copy(out=mv, in_=psM.ap[:, 0:96]), reads=[psM.r], writes=[modv.r])
                else:
                    P.op("dve", lambda e, mv=mv: e.tensor_tensor(out=mv, in0=psM.ap[:, 0:96], in1=mv, op=ALU.add),
                         reads=[psM.r, modv.r], writes=[modv.r])
                yield
            for m in range(2):
                P.op("dve", lambda e, l=l, m=m: e.tensor_tensor(out=modv.ap[:, l, :, m], in0=modv.ap[:, l, :, m], in1=bmod_s.ap[:, l, :], op=ALU.add),
                     reads=[modv.r, bmod_s.r], writes=[modv.r])
                P.op("dve", lambda e, l=l, m=m: e.tensor_scalar(out=Amod.ap[:, l, m, :], in0=modv.ap[:, l, 16:32, m], scalar1=1.0, scalar2=None, op0=ALU.add),
                     reads=[modv.r], writes=[Amod.r])
                P.op("dve", lambda e, l=l, m=m: e.tensor_tensor(out=Amod.ap[:, l, m, :], in0=Amod.ap[:, l, m, :], in1=normg_s.ap[:, l, :], op=ALU.mult),
                     reads=[Amod.r, normg_s.r], writes=[Amod.r])
                P.op("pool", lambda e, l=l, m=m: e.dma_start(out=modrow[2 * l + m].rearrange("(k p) -> p k", p=128), in_=modv.ap[:, l, 32:48, m],
                                                          allow_slow_non_contiguous=True),
                     reads=[modv.r], writes=[wres["modrow"]], chan="modrow")


    def cast_win(l, wf, wb):
        engs = ["dve", "act", "pool"]
        wk = "wbf%d" % l
        for kc in range(16):
            f_ = wf.next()
            b_ = wb.next()
            dma_in(f_, f_.ap, w_in[l, kc * 128:(kc + 1) * 128, :], [])
            for part in range(4):
                eg = engs[(kc * 4 + part) % 3]
                sl = slice(part * 1664, (part + 1) * 1664)
                if eg == "act":
                    P.op("act", lambda e, f_=f_, b_=b_, sl=sl: e.activation(out=b_.ap[:, sl], in_=f_.ap[:, sl], func=AF.Copy),
                         reads=[f_.r], writes=[b_.r])
                else:
                    P.op(eg, lambda e, f_=f_, b_=b_, sl=sl: e.tensor_copy(out=b_.ap[:, sl], in_=f_.ap[:, sl]), reads=[f_.r], writes=[b_.r])
            P.op("sp", lambda e, b_=b_, kc=kc: e.dma_start(out=wbf[l][kc * 128:(kc + 1) * 128, :], in_=b_.ap),
                 reads=[b_.r], writes=[wres[wk]], chan=b_.chan + "s")
            yield

    def interleave2(g1, g2):
        d1 = d2 = False
        while not (d1 and d2):
            if not d1:
                try:
                    next(g1)
                except StopIteration:
                    d1 = True
            if not d2:
                try:
                    next(g2)
                except StopIteration:
                    d2 = True

    wf0 = Rot([A([PW], F32, "wf") for _ in range(2)])
    wb0 = Rot([A([PW], BF16, "wb") for _ in range(2)])
    interleave2(mod_gen(), cast_win(0, wf0, wb0))

    def norm_tile(seg, l, tok0, hT, col0, pools, hres):
        xt = pools["xt"].next()
        junk = pools["junk"]
        xn = pools["xn"].next()
        ss = pools["ss"].next()
        pT = pools["pT"].next()
        xs = seg.x[l]
        dma_in(xt, xt.ap, xs.ap[tok0:tok0 + 128, :], xs.res(tok0, tok0 + 128))
        P.op("pool", lambda e: e.memset(ss.ap, 0.0), writes=[ss.r])
        P.op("act", lambda e: e.activation(out=junk.ap, in_=xt.ap, func=AF.Square, accum_out=ss.ap[:, 0:1]),
             reads=[xt.r, ss.r], writes=[ss.r, junk.r])
        NT = int(os.environ.get('NT', '9'))
        if NT < 2:
            return
        P.op("act", lambda e: e.activation(out=ss.ap[:, 1:2], in_=ss.ap[:, 0:1], func=AF.Ln, scale=1.0 / D, bias=eps6.ap[:, 0:1]),
             reads=[ss.r, eps6.r], writes=[ss.r])
        P.op("act", lambda e: e.activation(out=ss.ap[:, 2:3], in_=ss.ap[:, 1:2], func=AF.Exp, scale=-0.5), reads=[ss.r], writes=[ss.r])
        if NT < 3:
            return
        P.op("dve", lambda e: e.tensor_scalar(out=xn.ap, in0=xt.ap, scalar1=ss.ap[:, 2:3], scalar2=None, op0=ALU.mult),
             reads=[xt.r, ss.r], writes=[xn.r])
        if NT < 4:
            return
        pTv = pT.ap.bitcast(BF16).rearrange("p (a b) -> p a b", b=128)

        def tr(e):
            for k in range(16):
                i = e.transpose(out=pTv[:, k, :], in_=xn.ap[:, k * 128:(k + 1) * 128], identity=ident.ap)
            return i
        P.op("pe", tr, reads=[xn.r, ident.r], writes=[pT.r])
        if NT < 5:
            return
        m = seg.midx
        for k in range(16):
            dst = hT.ap[:, k, col0:col0 + 128]
            EV = os.environ.get('EV', 'act')
            if (k % 2 == 0 and EV != 'dve') or EV == 'act':
                P.op("act", lambda e, k=k, dst=dst: e.activation(out=dst, in_=pTv[:, k, :], func=AF.Identity,
                                                                 scale=Amod.ap[:, l, m, k:k + 1], bias=modv.ap[:, l, k, m:m + 1]),
                     reads=[pT.r, Amod.r, modv.r], writes=[hres[k % 2]])
            else:
                P.op("dve", lambda e, k=k, dst=dst: e.tensor_scalar(out=dst, in0=pTv[:, k, :], scalar1=Amod.ap[:, l, m, k:k + 1],
                                                                    scalar2=modv.ap[:, l, k, m:m + 1], op0=ALU.mult, op1=ALU.add),
                     reads=[pT.r, Amod.r, modv.r], writes=[hres[k % 2]])

    def layer(l):
        P.barrier()
        ar.top = persist_top
        wf = Rot([A([2048], F32, "wf") for _ in range(2)])
        f_ = wf.next()
        dma_in(f_, f_.ap[:, 0:512].rearrange("p (g d) -> p g d", g=4), w_pool[l].rearrange("g c d -> c g d"), [])
        P.op("dve", lambda e, f_=f_: e.tensor_copy(out=wpool_s.ap.rearrange("p g d -> p (g d)"), in_=f_.ap[:, 0:512]), reads=[f_.r], writes=[wpool_s.r])
        f_ = wf.next()
        dma_in(f_, f_.ap[:, 0:2048].rearrange("p (a d) -> p a d", a=4), conv_pw[l].rearrange("(a c) d -> c a d", c=128), [])
        P.op("dve", lambda e, f_=f_: e.tensor_copy(out=cpw_s.ap.rearrange("p a d -> p (a d)"), in_=f_.ap[:, 0:2048]), reads=[f_.r], writes=[cpw_s.r])
        chk(1 + 10 * l)
        P.barrier()
        ar.top = persist_top
        pB = {
            "xt": Rot([A([D], F32, "xt") for _ in range(2)]),
            "junk": A([D], BF16, "junk"),
            "xn": Rot([A([D], BF16, "xn") for _ in range(2)]),
            "ss": Rot([A([4], F32, "ss") for _ in range(2)]),
            "pT": Rot(psS),
        }
        hT = A([16, 512], BF16, "hT")
        hTres = [[Res("hT") for _ in range(2)] for _ in range(4)]
        hTall = [r_ for rr in hTres for r_ in rr]
        wts = Rot([A([16, 512], BF16, "wt") for _ in range(3)])
        wstate = {"g": -1, "wt": None}
        wv = A([16, 1024], BF16, "wv")
        cos_s = A([512], F32, "cos")
        sin_s = A([512], F32, "sin")
        msk_s = A([512], F32, "msk")
        stg = Rot([A([512], BF16, "stg") for _ in range(4)])
        qbs = Rot([A([512], BF16, "qb") for _ in range(2)])
        t1s = Rot([A([512], F32, "t1") for _ in range(2)])
        t2s = Rot([A([512], F32, "t2") for _ in range(2)])
        sg = A([4, 512], F32, "sg")
        vst = Rot([A([8, 65], BF16, "vst") for _ in range(2)])
        for v_ in vst.t:
            P.op("pool", lambda e, v_=v_: e.memset(v_.ap[:, :, 64:65], 1.0), writes=[v_.r])

        auxf = A([3072], F32, "auxf")
        auxb = A([3072], BF16, "auxb")

        def aux_casts():
            for fc in range(16):
                dma_in(auxf, auxf.ap[:, 0:D], w_out[l, fc * 128:(fc + 1) * 128, :], [])
                P.op("pool", lambda e: e.tensor_copy(out=auxb.ap[:, 0:D], in_=auxf.ap[:, 0:D]), reads=[auxf.r], writes=[auxb.r])
                P.op("pool", lambda e, fc=fc: e.dma_start(out=woutbf[l][fc], in_=auxb.ap[:, 0:D]), reads=[auxb.r], writes=[wres["wout%d" % l]], chan=auxb.chan + "o")
                yield
            n_ = 4 * NSLOT * 128
            for c in range(5):
                for q4 in range(4):
                    dma_in(auxf, auxf.ap[:, 0:n_], ebraw[l, c, :, q4 * n_:(q4 + 1) * n_], [])
                    P.op("act", lambda e: e.activation(out=auxb.ap[:, 0:n_], in_=auxf.ap[:, 0:n_], func=AF.Exp), reads=[auxf.r], writes=[auxb.r])
                    P.op("pool", lambda e, c=c, q4=q4: e.dma_start(out=ebs[l][c, :, q4 * n_:(q4 + 1) * n_], in_=auxb.ap[:, 0:n_]),
                         reads=[auxb.r], writes=[wres["ebs%d" % l]], chan=auxb.chan + "o")
                    yield
        auxg = aux_casts()
        auxc = {"n": 0}

        def passB(seg, tok0, T, kinds):
            nt = T // 128
            wstate["g"] = -1
            for ti in range(nt):
                norm_tile(seg, l, tok0 + ti * 128, hT, ti * 128, pB, hTres[ti])
            chk(5)
            if seg.rope:
                dma_in(cos_s, cos_s.ap[:, 0:T], cosT[:, tok0:tok0 + T], [])
                dma_in(sin_s, sin_s.ap[:, 0:T], sinT[:, tok0:tok0 + T], [])
            dma_in(msk_s, msk_s.ap[:, 0:T], seg.mask[:, tok0:tok0 + T], [])
            order = []
            if "u" in kinds:
                order += [(f, "u", f) for f in range(0, 4)]
            if "g" in kinds:
                order += [(f, "g", f - 4) for f in range(4, 8)]
            if "q" in kinds:
                order += [(f, "q", f - 8) for f in range(8, 16)]
            if "k" in kinds:
                order += [(f, "k", f - 16) for f in range(16, 24)]
            if "g" in kinds:
                order += [(f, "g", f - 32 + 4) for f in range(32, 40)]
            if "y" in kinds:
                order += [(f, "sg", f - 44) for f in range(44, 48)]
                order += [(f, "y", f - 40) for f in range(40, 44)]
            if "g" in kinds:
                order += [(f, "g", f - 48 + 12) for f in range(48, 52)]
            pend = []

            def flush():
                while pend:
                    pend.pop(0)()
            for (f, kind, idx) in order:
                if wstate["g"] != f // 4:
                    wstate["g"] = f // 4
                    wstate["wt"] = wts.next()
                    g0 = (f // 4) * 512
                    dma_in(wstate["wt"], wstate["wt"].ap, wbf[l][:, g0:g0 + 512].rearrange("(k p) c -> p k c", p=128), [wres["wbf%d" % l]])
                wt = wstate["wt"]
                fo = (f % 4) * 128
                ps = psrot.next()

                def mm(e, wt=wt, ps=ps, fo=fo):
                    for kc in range(16):
                        i = e.matmul(ps.ap[:, 0:T], lhsT=wt.ap[:, kc, fo:fo + 128], rhs=hT.ap[:, kc, 0:T], start=(kc == 0), stop=(kc == 15))
                    return i
                P.op("pe", mm, reads=[wt.r] + hTall, writes=[ps.r])
                flush()
                auxc["n"] += 1
                if auxc["n"] % 8 == 0:
                    next(auxg, None)
                if kind == "u":
                    st = stg.next()
                    P.op("dve", lambda e, st=st, ps=ps: e.tensor_tensor(out=st.ap[:, 0:T], in0=ps.ap[:, 0:T], in1=msk_s.ap[:, 0:T], op=ALU.mult),
                         reads=[ps.r, msk_s.r], writes=[st.r])
                    dma_out(st, st.ap[:, 0:T], seg.uT.ap[idx, :, tok0:tok0 + T], seg.uT.res(tok0, tok0 + T))
                elif kind == "g":
                    st = stg.next()
                    P.op("act", lambda e, st=st, ps=ps: e.activation(out=st.ap[:, 0:T], in_=ps.ap[:, 0:T], func=AF.Silu), reads=[ps.r], writes=[st.r])
                    dma_out(st, st.ap[:, 0:T], seg.gate.ap[idx, :, tok0:tok0 + T], seg.gate.res(tok0, tok0 + T))
                elif kind in ("q", "k"):
                    dstS = seg.qT if kind == "q" else seg.kT
                    st = stg.next()
                    if not seg.rope:
                        P.op("act", lambda e, st=st, ps=ps: e.activation(out=st.ap[:, 0:T], in_=ps.ap[:, 0:T], func=AF.Copy), reads=[ps.r], writes=[st.r])
                    else:
                        qb = qbs.next()
                        t1 = t1s.next()
                        t2 = t2s.next()
                        pr = psB[0]
                        P.op("act", lambda e, qb=qb, ps=ps: e.activation(out=qb.ap[:, 0:T], in_=ps.ap[:, 0:T], func=AF.Copy), reads=[ps.r], writes=[qb.r])
                        P.op("dve", lambda e, t1=t1, ps=ps: e.tensor_tensor(out=t1.ap[:, 0:T], in0=ps.ap[:, 0:T], in1=cos_s.ap[:, 0:T], op=ALU.mult),
                             reads=[ps.r, cos_s.r, qb.r], writes=[t1.r])

                        def rope_rest(qb=qb, t1=t1, t2=t2, pr=pr, st=st, dstS=dstS, idx=idx):
                            P.op("pe", lambda e: e.matmul(pr.ap[:, 0:T], lhsT=perm.ap, rhs=qb.ap[:, 0:T], start=True, stop=True),
                                 reads=[perm.r, qb.r], writes=[pr.r])
                            P.op("dve", lambda e: e.tensor_tensor(out=t2.ap[:, 0:T], in0=pr.ap[:, 0:T], in1=sin_s.ap[:, 0:T], op=ALU.mult),
                                 reads=[pr.r, sin_s.r], writes=[t2.r])
                            P.op("dve", lambda e: e.tensor_tensor(out=st.ap[:, 0:T], in0=t1.ap[:, 0:T], in1=t2.ap[:, 0:T], op=ALU.add),
                                 reads=[t1.r, t2.r], writes=[st.r])
                            dma_out(st, st.ap[:, 0:T], dstS.ap[idx, :, tok0:tok0 + T], dstS.res(tok0, tok0 + T))
                        pend.append(rope_rest)
                        continue
                    dma_out(st, st.ap[:, 0:T], dstS.ap[idx, :, tok0:tok0 + T], dstS.res(tok0, tok0 + T))
                elif kind == "sg":
                    P.op("act", lambda e, ps=ps, idx=idx: e.activation(out=sg.ap[:, idx, 0:T], in_=ps.ap[:, 0:T], func=AF.Sigmoid), reads=[ps.r], writes=[sg.r])
                    P.op("pool", lambda e, idx=idx: e.tensor_tensor(out=sg.ap[:, idx, 0:T], in0=sg.ap[:, idx, 0:T], in1=msk_s.ap[:, 0:T], op=ALU.mult),
                         reads=[sg.r, msk_s.r], writes=[sg.r])
                elif kind == "y":
                    st = stg.next()
                    P.op("dve", lambda e, st=st, ps=ps, idx=idx: e.tensor_tensor(out=st.ap[:, 0:T], in0=ps.ap[:, 0:T], in1=sg.ap[:, idx, 0:T], op=ALU.mult),
                         reads=[ps.r, sg.r], writes=[st.r])
                    dma_out(st, st.ap[:, 0:T], seg.yT.ap[idx, :, tok0:tok0 + T], seg.yT.res(tok0, tok0 + T))
            flush()
            chk(6)
            if "v" in kinds:
                dma_in(wv, wv.ap, wbf[l][:, 3072:4096].rearrange("(k p) c -> p k c", p=128), [wres["wbf%d" % l]])
                for ti in range(nt):
                    for half in range(2):
                        ps = psrot.next()

                        def mmv(e, ps=ps, ti=ti, half=half):
                            for kc in range(16):
                                i = e.matmul(ps.ap, lhsT=hT.ap[:, kc, ti * 128:(ti + 1) * 128],
                                             rhs=wv.ap[:, kc, half * 512:(half + 1) * 512], start=(kc == 0), stop=(kc == 15))
                            return i
                        P.op("pe", mmv, reads=[wv.r] + hTall, writes=[ps.r])
                        vs = vst.next()
                        if half == 0:
                            P.op("act", lambda e, vs=vs, ps=ps: e.activation(out=vs.ap[:, :, 0:64], in_=ps.ap.rearrange("p (h d) -> p h d", h=8), func=AF.Copy), reads=[ps.r], writes=[vs.r])
                        else:
                            P.op("dve", lambda e, vs=vs, ps=ps: e.tensor_copy(out=vs.ap[:, :, 0:64], in_=ps.ap.rearrange("p (h d) -> p h d", h=8)), reads=[ps.r], writes=[vs.r])
                        t0 = tok0 + ti * 128
                        dma_out(vs, vs.ap, seg.v.ap[t0:t0 + 128, half * 520:(half + 1) * 520].rearrange("t (h d) -> t h d", h=8), seg.v.res(t0, t0 + 128))

        chk(8)
        ALLK = ("u", "g", "q", "k", "v", "y")
        if stop == 7:
            passB(cseg, 0, CTX, ALLK)
            chk(7)
        HALO = ("u", "k", "v", "y")
        passB(cseg, 0, CTX, ALLK if l == 0 else ("k", "v"))
        dma_in(ctxK, ctxK.ap, cseg.kT.ap.rearrange("c p t -> p c t"), cseg.kT.res(0, CTX))
        for a in range(2):
            P.op("sp", lambda e, a=a: e.dma_start(out=ctxV.ap[:, a], in_=cseg.v.ap[a * 128:(a + 1) * 128, :].rearrange("t (h d) -> t h d", h=H)),
                 reads=cseg.v.res(a * 128, (a + 1) * 128), writes=[ctxV.r], chan=ctxV.chan)
        chk(2 + 10 * l)
        if l == 0:
            r_lo, r_hi = 4, OWN + 12
        else:
            r_lo, r_hi = 8, OWN + 8
        passB(lat, (r_lo - 4) * GW, 4 * GW, HALO)
        for r in range(r_lo, r_hi, 8):
            passB(lat, r * GW, 8 * GW, ALLK)
        passB(lat, r_hi * GW, 4 * GW, HALO)

        for _ in auxg:
            pass
        chk(3 + 10 * l)
        P.barrier()
        ar.top = persist_top
        TB = 256
        Kw = A([8, 8 * 128], BF16, "Kw")
        Vw = A([8, H, 65], BF16, "Vw")
        ebsp = Rot([A([2, NSLOT * 128], BF16, "ebsp") for _ in range(3)])
        Pts = Rot([A([8 * 128], BF16, "Pt") for _ in range(3)])
        nas = Rot([A([1024], BF16, "na") for _ in range(2)])
        rdn = Rot([A([4], F32, "rdn") for _ in range(4)])
        gtsAC = A([8, TB], BF16, "gtsAC")
        bufs = [dict(mixT=A([16, TB], BF16, "mixT"), gts=gtsAC, gB=A([8, TB], BF16, "gB"), Qb=A([8, TB], BF16, "Qb"),
                     mixA=Res("mixA"), mixB=Res("mixB"), mixC=Res("mixC")) for _ in range(2)]
        ubuf = A([4, TB + 16], BF16, "ubuf")
        mbuf = A([TB + 16], F32, "mbuf")
        tmpA = [A([TB + 16], F32, "tmpA%d" % i) for i in range(4)]
        icn = A([4, TB], F32, "icn")
        pooled = A([4, TB], BF16, "pooled")
        ybuf = A([4, TB + 30], BF16, "ybuf")
        cacc = A([4, TB], F32, "cacc")
        caccr = [Res("cacc%d" % c) for c in range(4)]
        csq = A([TB], F32, "csq")
        cmean = A([TB], F32, "cmean")
        cvar = A([TB], F32, "cvar")
        cd = A([TB], F32, "cd")
        cz = A([4, TB], BF16, "cz")
        xin = Rot([A([512], F32, "xin") for _ in range(2)])
        xnew = Rot([A([D], F32, "xnew") for _ in range(2)])
        gbc = A([D], F32, "gbc")
        fbc = A([D], F32, "fbc") if l == 1 else None
        junkC = A([D], BF16, "junkC")
        ssC = Rot([A([4], F32, "ssC") for _ in range(2)])
        psOr = Rot([psB[0], psA[2]])
        psG = Rot([psA[0], psA[1]])
        psSr = Rot(psS)
        if l == 1:
            P.op("sp", lambda e: e.dma_start(out=fbc.ap, in_=fng.broadcast_to([128, D])), writes=[fbc.r], chan=fbc.chan)

        def halo_load(dst, scr, tok0, T, hl, ntok, is_mask=False):
            lo = max(tok0 - hl, 0)
            hi = min(tok0 + T + hl, ntok)
            a = lo - (tok0 - hl)
            b = a + (hi - lo)
            W = T + 2 * hl
            if is_mask:
                if a > 0:
                    P.op("pool", lambda e: e.memset(dst.ap[:, 0:a], 0.0), writes=[dst.r])
                if b < W:
                    P.op("pool", lambda e: e.memset(dst.ap[:, b:W], 0.0), writes=[dst.r])
                dma_in(dst, dst.ap[:, a:b], scr[:, lo:hi], [])
            else:
                if a > 0:
                    P.op("pool", lambda e: e.memset(dst.ap[:, :, 0:a], 0.0), writes=[dst.r])
                if b < W:
                    P.op("pool", lambda e: e.memset(dst.ap[:, :, b:W], 0.0), writes=[dst.r])
                dma_in(dst, dst.ap[:, :, a:b], scr.ap[:, :, lo:hi].rearrange("c p t -> p c t"), scr.res(lo, hi))

        def attn_loads(seg, tok0, T, bf):
            gB, Qb = bf["gB"], bf["Qb"]
            npair = T // 128
            n0 = tok0 // 128
            dma_in(gB, gB.ap[:, :, 0:T], seg.gate.ap[4:12, :, tok0:tok0 + T].rearrange("c p t -> p c t"), seg.gate.res(tok0, tok0 + T))
            dma_in(Qb, Qb.ap[:, :, 0:T], seg.qT.ap[:, :, tok0:tok0 + T].rearrange("c p t -> p c t"), seg.qT.res(tok0, tok0 + T))
            if seg is lat:
                kp_lo = max(n0 - 3, 0)
                kp_hi = min(n0 + npair + 3, NP)
                nk = kp_hi - kp_lo
                assert nk <= 8
                dma_in(Kw, Kw.ap[:, :, 0:nk * 128], seg.kT.ap[:, :, kp_lo * 128:kp_hi * 128].rearrange("c p t -> p c t"),
                       seg.kT.res(kp_lo * 128, kp_hi * 128))
                dma_in(Vw, Vw.ap[:, 0:nk], seg.v.ap[kp_lo * 128:kp_hi * 128, :].rearrange("(s t) (h d) -> t s h d", t=128, h=H),
                       seg.v.res(kp_lo * 128, kp_hi * 128))

        def attention(seg, tok0, T, bf):
            mixT, gB, Qb = bf["mixT"], bf["gB"], bf["Qb"]
            npair = T // 128
            n0 = tok0 // 128
            is_lat = seg is lat
            kp_lo = max(n0 - 3, 0) if is_lat else 0
            own_lo = 4
            own_hi = 4 + OWN // 2
            for pi in range(npair):
                n = n0 + pi
                cls = 0
                if is_lat:
                    if n == own_lo:
                        cls = 1
                    elif n == own_lo + 1:
                        cls = 2
                    elif n == own_hi - 2:
                        cls = 3
                    elif n == own_hi - 1:
                        cls = 4
                    rel = CLS_SLOTS[cls]
                    nloc = len(rel)
                else:
                    rel = []
                    nloc = 0
                nsl = nloc + 2
                na = nas.next()
                state = {}

                def qk(h, pi=pi, rel=rel, nloc=nloc, nsl=nsl, n=n, cls=cls, state=state):
                    S = psSr.next()
                    state[h] = S
                    ch, pb = h // 2, (h % 2) * 64
                    if nloc and h % 2 == 0:
                        ebt = ebsp.next()
                        state["ebt", h // 2] = ebt
                        off = h * NSLOT * 128
                        dma_in(ebt, ebt.ap, ebs[l][cls, :, off:off + 2 * NSLOT * 128].rearrange("p (a b) -> p a b", a=2), [wres["ebs%d" % l]])

                    def f(e):
                        for s in range(nsl):
                            if s < nloc:
                                ko = (n + rel[s] - kp_lo) * 128
                                lk = Kw.ap[pb:pb + 64, ch, ko:ko + 128]
                            else:
                                lk = ctxK.ap[pb:pb + 64, ch, (s - nloc) * 128:(s - nloc + 1) * 128]
                            i = e.matmul(S.ap[:, s * 128:(s + 1) * 128], lhsT=lk, rhs=Qb.ap[pb:pb + 64, ch, pi * 128:(pi + 1) * 128],
                                         start=True, stop=True)
                        return i
                    P.op("pe", f, reads=[Kw.r, ctxK.r, Qb.r] if is_lat else [ctxK.r, Qb.r], writes=[S.r])

                def rest(h, pi=pi, rel=rel, nloc=nloc, nsl=nsl, n=n, cls=cls, na=na, state=state):
                    S = state.pop(h)
                    Pt = Pts.next()
                    n1 = min(nsl, 4) * 128
                    P.op("act", lambda e: e.activation(out=Pt.ap[:, 0:n1], in_=S.ap[:, 0:n1], func=AF.Exp, scale=0.125), reads=[S.r], writes=[Pt.r])
                    if nsl > 4:
                        P.op("act", lambda e: e.activation(out=Pt.ap[:, n1:nsl * 128], in_=S.ap[:, n1:nsl * 128], func=AF.Exp, scale=0.125),
                             reads=[S.r], writes=[Pt.r])
                    if nloc:
                        ebt = state["ebt", h // 2]
                        eba = ebt.ap[:, h % 2, 0:nloc * 128]
                        P.op("dve" if h % 4 == 0 else "pool", lambda e: e.tensor_tensor(out=Pt.ap[:, 0:nloc * 128], in0=Pt.ap[:, 0:nloc * 128], in1=eba, op=ALU.mult),
                             reads=[Pt.r, ebt.r], writes=[Pt.r])
                    if h % 4 == 0:
                        state["O"] = psOr.next()
                    O_ = state["O"]
                    Ov = O_.ap.rearrange("p (a b) -> p a b", a=4)
                    oreg = Ov[:, h % 4, 0:65]

                    def pv(e):
                        for s in range(nsl):
                            if s < nloc:
                                rv = Vw.ap[:, n + rel[s] - kp_lo, h, :]
                            else:
                                rv = ctxV.ap[:, s - nloc, h, :]
                            i = e.matmul(oreg, lhsT=Pt.ap[:, s * 128:(s + 1) * 128], rhs=rv, start=(s == 0), stop=(s == nsl - 1))
                        return i
                    P.op("pe", pv, reads=[Pt.r, ctxV.r] + ([Vw.r] if is_lat else []), writes=[O_.r])
                    if h % 4 == 3:
                        rd = rdn.next()
                        g4 = h // 4
                        P.op("dve", lambda e: e.reciprocal(out=rd.ap[:, 0:4], in_=Ov[:, :, 64]), reads=[O_.r], writes=[rd.r])
                        P.op("dve", lambda e: e.tensor_tensor(out=na.ap[:, g4 * 256:(g4 + 1) * 256].rearrange("p (a b) -> p a b", a=4), in0=Ov[:, :, 0:64],
                                                              in1=rd.ap[:, 0:4].unsqueeze(2).broadcast_to([128, 4, 64]), op=ALU.mult),
                             reads=[O_.r, rd.r], writes=[na.r])
                qk(0)
                for h in range(H):
                    if h + 1 < H:
                        qk(h + 1)
                    rest(h)
                    yield
                psT = psG.next()
                pTv = psT.ap.bitcast(BF16).rearrange("p (a b) -> p a b", b=128)[:, 0:8, :]

                def tr(e, na=na, pTv=pTv):
                    for c in range(8):
                        i = e.transpose(out=pTv[:, c, :], in_=na.ap[:, c * 128:(c + 1) * 128], identity=ident.ap)
                    return i
                P.op("pe", tr, reads=[na.r, ident.r], writes=[psT.r])
                P.op("dve", lambda e, pi=pi, pTv=pTv: e.tensor_tensor(out=mixT.ap[:, 4:12, pi * 128:(pi + 1) * 128], in0=pTv,
                                                             in1=gB.ap[:, :, pi * 128:(pi + 1) * 128], op=ALU.mult),
                     reads=[psT.r, gB.r], writes=[bf["mixB"]])
                yield

        def pool_mix(seg, tok0, T, bf):
            mixT, gts = bf["mixT"], bf["gts"]
            halo_load(ubuf, seg.uT, tok0, T, 8, seg.ntok)
            halo_load(mbuf, seg.mask, tok0, T, 8, seg.ntok, is_mask=True)
            W = T + 16
            m2, m4, m8, tt = tmpA
            P.op("dve", lambda e: e.tensor_tensor(out=m2.ap[:, 0:W - 1], in0=mbuf.ap[:, 0:W - 1], in1=mbuf.ap[:, 1:W], op=ALU.add), reads=[mbuf.r], writes=[m2.r])
            P.op("dve", lambda e: e.tensor_tensor(out=m4.ap[:, 0:W - 3], in0=m2.ap[:, 0:W - 3], in1=m2.ap[:, 2:W - 1], op=ALU.add), reads=[m2.r], writes=[m4.r])
            P.op("dve", lambda e: e.tensor_tensor(out=m8.ap[:, 0:W - 7], in0=m4.ap[:, 0:W - 7], in1=m4.ap[:, 4:W - 3], op=ALU.add), reads=[m4.r], writes=[m8.r])
            srcs = [(mbuf, 1), (m2, 2), (m4, 4), (m8, 8)]
            for g, (sb, sh) in enumerate(srcs):
                P.op("dve", lambda e, g=g, sb=sb, sh=sh: e.tensor_tensor(out=icn.ap[:, g, 0:T], in0=sb.ap[:, 8 - sh:8 - sh + T], in1=sb.ap[:, 8:8 + T], op=ALU.add),
                     reads=[sb.r], writes=[icn.r])
            P.op("dve", lambda e: e.tensor_scalar(out=icn.ap[:, :, 0:T], in0=icn.ap[:, :, 0:T], scalar1=1.0, scalar2=None, op0=ALU.max), reads=[icn.r], writes=[icn.r])
            P.op("dve", lambda e: e.reciprocal(out=icn.ap[:, :, 0:T], in_=icn.ap[:, :, 0:T]), reads=[icn.r], writes=[icn.r])
            yield
            for g in range(4):
                u = ubuf.ap[:, g, :]
                a2, a4, a8, s_ = tmpA
                if g >= 1:
                    P.op("dve", lambda e, u=u: e.tensor_tensor(out=a2.ap[:, 0:W - 1], in0=u[:, 0:W - 1], in1=u[:, 1:W], op=ALU.add), reads=[ubuf.r], writes=[a2.r])
                if g >= 2:
                    P.op("dve", lambda e: e.tensor_tensor(out=a4.ap[:, 0:W - 3], in0=a2.ap[:, 0:W - 3], in1=a2.ap[:, 2:W - 1], op=ALU.add), reads=[a2.r], writes=[a4.r])
                if g >= 3:
                    P.op("dve", lambda e: e.tensor_tensor(out=a8.ap[:, 0:W - 7], in0=a4.ap[:, 0:W - 7], in1=a4.ap[:, 4:W - 3], op=ALU.add), reads=[a4.r], writes=[a8.r])
                sh = [1, 2, 4, 8][g]
                srcT = [ubuf, a2, a4, a8][g]
                sap = u if g == 0 else srcT.ap
                P.op("dve", lambda e, sap=sap, sh=sh: e.tensor_tensor(out=s_.ap[:, 0:T], in0=sap[:, 8 - sh:8 - sh + T], in1=sap[:, 8:8 + T], op=ALU.add),
                     reads=[srcT.r], writes=[s_.r])
                P.op("dve", lambda e, g=g: e.tensor_tensor(out=s_.ap[:, 0:T], in0=s_.ap[:, 0:T], in1=icn.ap[:, g, 0:T], op=ALU.mult), reads=[s_.r, icn.r], writes=[s_.r])
                P.op("dve", lambda e, g=g, u=u: e.tensor_tensor(out=pooled.ap[:, g, 0:T], in0=s_.ap[:, 0:T], in1=u[:, 8:8 + T], op=ALU.subtract),
                     reads=[s_.r, ubuf.r], writes=[pooled.r])
                ps = psG.next()
                P.op("pe", lambda e, g=g, ps=ps: e.matmul(ps.ap[:, 0:T], lhsT=wpool_s.ap[:, g, :], rhs=pooled.ap[:, g, 0:T], start=True, stop=True),
                     reads=[wpool_s.r, pooled.r], writes=[ps.r])
                P.op("dve", lambda e, g=g, ps=ps: e.scalar_tensor_tensor(out=mixT.ap[:, g, 0:T], in0=ps.ap[:, 0:T], scalar=pscale_s.ap[:, l, g:g + 1],
                                                                         in1=gts.ap[:, g, 0:T], op0=ALU.mult, op1=ALU.mult),
                     reads=[ps.r, pscale_s.r, gts.r], writes=[bf["mixA"]])
                yield

        def conv_mix(seg, tok0, T, bf):
            mixT, gts = bf["mixT"], bf["gts"]
            halo_load(ybuf, seg.yT, tok0, T, 15, seg.ntok)
            for c in range(4):
                acc = cacc.ap[:, c, 0:T]
                P.op("dve", lambda e, c=c, acc=acc: e.tensor_scalar(out=acc, in0=ybuf.ap[:, c, 0:T], scalar1=cdw_s.ap[:, l, c, 0:1],
                                                                    scalar2=cvecs_s.ap[:, l, c, 0:1], op0=ALU.mult, op1=ALU.add),
                     reads=[ybuf.r, cdw_s.r, cvecs_s.r], writes=[caccr[c]])
            yield
            for j in range(1, 31):
                for c in range(4):
                    acc = cacc.ap[:, c, 0:T]
                    P.op("dve", lambda e, c=c, acc=acc, j=j: e.scalar_tensor_tensor(out=acc, in0=ybuf.ap[:, c, j:j + T], scalar=cdw_s.ap[:, l, c, j:j + 1],
                                                                                   in1=acc, op0=ALU.mult, op1=ALU.add),
                         reads=[ybuf.r, cdw_s.r, caccr[c]], writes=[caccr[c]])
                yield
            psm = psG.next()
            psq = psG.next()

            def st1(e):
                for c in range(4):
                    i = e.matmul(psm.ap[:, 0:T], lhsT=onesm.ap, rhs=cacc.ap[:, c, 0:T], start=(c == 0), stop=(c == 3))
                return i
            P.op("pe", st1, reads=[onesm.r] + caccr, writes=[psm.r])
            for c in range(4):
                P.op("act", lambda e, c=c: e.activation(out=csq.ap[:, 0:T], in_=cacc.ap[:, c, 0:T], func=AF.Square), reads=[caccr[c]], writes=[csq.r])
                P.op("pe", lambda e, c=c: e.matmul(psq.ap[:, 0:T], lhsT=onesm.ap, rhs=csq.ap[:, 0:T], start=(c == 0), stop=(c == 3)),
                     reads=[onesm.r, csq.r], writes=[psq.r], chan=None)
            yield
            P.op("act", lambda e: e.activation(out=cmean.ap[:, 0:T], in_=psm.ap[:, 0:T], func=AF.Copy), reads=[psm.r], writes=[cmean.r])
            P.op("dve", lambda e: e.tensor_tensor(out=cvar.ap[:, 0:T], in0=cmean.ap[:, 0:T], in1=cmean.ap[:, 0:T], op=ALU.mult), reads=[cmean.r], writes=[cvar.r])
            P.op("dve", lambda e: e.tensor_tensor(out=cvar.ap[:, 0:T], in0=psq.ap[:, 0:T], in1=cvar.ap[:, 0:T], op=ALU.subtract), reads=[psq.r, cvar.r], writes=[cvar.r])
            P.op("act", lambda e: e.activation(out=cvar.ap[:, 0:T], in_=cvar.ap[:, 0:T], func=AF.Ln, bias=eps5.ap[:, 0:1]), reads=[cvar.r, eps5.r], writes=[cvar.r])
            P.op("act", lambda e: e.activation(out=cvar.ap[:, 0:T], in_=cvar.ap[:, 0:T], func=AF.Exp, scale=-0.5), reads=[cvar.r], writes=[cvar.r])
            yield
            for c in range(4):
                P.op("dve", lambda e, c=c: e.tensor_tensor(out=cd.ap[:, 0:T], in0=cacc.ap[:, c, 0:T], in1=cmean.ap[:, 0:T], op=ALU.subtract),
                     reads=[caccr[c], cmean.r], writes=[cd.r])
                P.op("dve", lambda e: e.tensor_tensor(out=cd.ap[:, 0:T], in0=cd.ap[:, 0:T], in1=cvar.ap[:, 0:T], op=ALU.mult), reads=[cd.r, cvar.r], writes=[cd.r])
                P.op("act", lambda e, c=c: e.activation(out=cz.ap[:, c, 0:T], in_=cd.ap[:, 0:T], func=AF.Silu, scale=cvecs_s.ap[:, l, c, 1:2], bias=cvecs_s.ap[:, l, c, 2:3]),
                     reads=[cd.r, cvecs_s.r], writes=[cz.r])
                yield
            for dch in range(4):
                ps = psG.next()

                def pw(e, ps=ps, dch=dch):
                    for c in range(4):
                        i = e.matmul(ps.ap[:, 0:T], lhsT=cpw_s.ap[:, c, dch * 128:(dch + 1) * 128], rhs=cz.ap[:, c, 0:T], start=(c == 0), stop=(c == 3))
                    return i
                P.op("pe", pw, reads=[cpw_s.r, cz.r], writes=[ps.r])
                P.op("dve", lambda e, ps=ps, dch=dch: e.scalar_tensor_tensor(out=mixT.ap[:, 12 + dch, 0:T], in0=ps.ap[:, 0:T], scalar=cvecs_s.ap[:, l, dch, 3:4],
                                                                             in1=gts.ap[:, 4 + dch, 0:T], op0=ALU.add, op1=ALU.mult),
                     reads=[ps.r, cvecs_s.r, gts.r], writes=[bf["mixC"]])
                yield

        def out_proj(seg, tok0, T, bf, final, out_tok0):
            mixT = bf["mixT"]
            nt = T // 128
            xs = seg.x[l]
            xns = [xnew.next() for _ in range(nt)]
            for cg in range(4):
                wo = wos.next()
                dma_in(wo, wo.ap, woutbf[l][:, :, cg * 512:(cg + 1) * 512].rearrange("f p c -> p f c"), [wres["wout%d" % l]])
                for ti in range(nt):
                    xn_ = xns[ti]
                    t0 = tok0 + ti * 128
                    xi = xin.next()
                    dma_in(xi, xi.ap, xs.ap[t0:t0 + 128, cg * 512:(cg + 1) * 512], xs.res(t0, t0 + 128))
                    ps = psG.next()

                    def mm(e, ps=ps, ti=ti, wo=wo):
                        for fc in range(16):
                            i = e.matmul(ps.ap, lhsT=mixT.ap[:, fc, ti * 128:(ti + 1) * 128], rhs=wo.ap[:, fc, :],
                                         start=(fc == 0), stop=(fc == 15))
                        return i
                    P.op("pe", mm, reads=[bf["mixA"], bf["mixB"], bf["mixC"], wo.r], writes=[ps.r])
                    dst = xn_.ap[:, cg * 512:(cg + 1) * 512]
                    P.op("dve", lambda e, ps=ps, dst=dst, cg=cg: e.tensor_tensor(out=dst, in0=ps.ap, in1=gbc.ap[:, cg * 512:(cg + 1) * 512], op=ALU.mult),
                         reads=[ps.r, gbc.r], writes=[xn_.r])
                    P.op("pool", lambda e, dst=dst, xi=xi: e.tensor_tensor(out=dst, in0=dst, in1=xi.ap, op=ALU.add), reads=[xn_.r, xi.r], writes=[xn_.r])
                    yield
            for ti in range(nt):
                xn_ = xns[ti]
                t0 = tok0 + ti * 128
                if not final:
                    x1 = seg.x[1]
                    dma_out(xn_, xn_.ap, x1.ap[t0:t0 + 128, :], x1.res(t0, t0 + 128))
                else:
                    ss = ssC.next()
                    P.op("pool", lambda e, ss=ss: e.memset(ss.ap, 0.0), writes=[ss.r])
                    P.op("act", lambda e, ss=ss, xn_=xn_: e.activation(out=junkC.ap, in_=xn_.ap, func=AF.Square, accum_out=ss.ap[:, 0:1]),
                         reads=[xn_.r, ss.r], writes=[ss.r, junkC.r])
                    P.op("act", lambda e, ss=ss: e.activation(out=ss.ap[:, 1:2], in_=ss.ap[:, 0:1], func=AF.Ln, scale=1.0 / D, bias=eps6.ap[:, 0:1]),
                         reads=[ss.r, eps6.r], writes=[ss.r])
                    P.op("act", lambda e, ss=ss: e.activation(out=ss.ap[:, 2:3], in_=ss.ap[:, 1:2], func=AF.Exp, scale=-0.5), reads=[ss.r], writes=[ss.r])
                    P.op("dve", lambda e, ss=ss, xn_=xn_: e.scalar_tensor_tensor(out=xn_.ap, in0=xn_.ap, scalar=ss.ap[:, 2:3], in1=fbc.ap, op0=ALU.mult, op1=ALU.mult),
                         reads=[xn_.r, ss.r, fbc.r], writes=[xn_.r])
                    o0 = out_tok0 + ti * 128
                    dma_out(xn_, xn_.ap, out[o0:o0 + 128, :], outres.res(o0, o0 + 128))
                yield

        wos = Rot([A([16, 512], BF16, "wo") for _ in range(2)])
        gstate = {"m": -1}

        def mixM(blk, bf):
            seg, tok0, T, final, out_tok0 = blk
            P.op("sp", lambda e: e.dma_start(out=gtsAC.ap[:, 0:4, 0:T], in_=seg.gate.ap[0:4, :, tok0:tok0 + T].rearrange("c p t -> p c t")),
                 reads=seg.gate.res(tok0, tok0 + T), writes=[gtsAC.r], chan=gtsAC.chan)
            P.op("sp", lambda e: e.dma_start(out=gtsAC.ap[:, 4:8, 0:T], in_=seg.gate.ap[12:16, :, tok0:tok0 + T].rearrange("c p t -> p c t")),
                 reads=seg.gate.res(tok0, tok0 + T), writes=[gtsAC.r], chan=gtsAC.chan)
            yield from pool_mix(seg, tok0, T, bf)
            yield from conv_mix(seg, tok0, T, bf)

        def mixO(blk, bf):
            seg, tok0, T, final, out_tok0 = blk
            m = seg.midx
            if gstate["m"] != m:
                P.op("sp", lambda e: e.dma_start(out=gbc.ap, in_=modrow[2 * l + m:2 * l + m + 1, :].broadcast_to([128, D])),
                     reads=[wres["modrow"]], writes=[gbc.r], chan=gbc.chan)
                gstate["m"] = m
            yield from out_proj(seg, tok0, T, bf, final, out_tok0)

        blocks = [(lat, r * GW, 4 * GW, l == 1, (r - 8) * GW) for r in range(r_lo, r_hi, 4)]
        if l == 0:
            blocks.append((cseg, 0, CTX, False, 0))
        def cast_next_layer():
            cf = A([1664], F32, "cf")
            cb = A([1664], BF16, "cb")
            for kc in range(16):
                for part in range(4):
                    sl = slice(part * 1664, (part + 1) * 1664)
                    dma_in(cf, cf.ap, w_in[1, kc * 128:(kc + 1) * 128, sl], [])
                    P.op("pool", lambda e: e.tensor_copy(out=cb.ap, in_=cf.ap), reads=[cf.r], writes=[cb.r])
                    P.op("pool", lambda e, kc=kc, sl=sl: e.dma_start(out=wbf[1][kc * 128:(kc + 1) * 128, sl], in_=cb.ap),
                         reads=[cb.r], writes=[wres["wbf1"]], chan=cb.chan + "o")
                    yield
        auxg2 = cast_next_layer() if l == 0 else iter(())
        auxn = {"n": 0}
        def step(g):
            try:
                next(g)
                return False
            except StopIteration:
                return True

        attn_loads(blocks[0][0], blocks[0][1], blocks[0][2], bufs[0])
        ga = attention(blocks[0][0], blocks[0][1], blocks[0][2], bufs[0])
        gm = mixM(blocks[0], bufs[0])
        da = dm = False
        first = True
        while not (da and dm):
            if not da:
                da = step(ga)
                if da and len(blocks) > 1:
                    attn_loads(blocks[1][0], blocks[1][1], blocks[1][2], bufs[1])
            if not dm:
                dm = step(gm)
        for bi, blk in enumerate(blocks):
            go = mixO(blk, bufs[bi % 2])
            if bi + 1 < len(blocks):
                nb = blocks[bi + 1]
                ga = attention(nb[0], nb[1], nb[2], bufs[(bi + 1) % 2])
                gm = mixM(nb, bufs[(bi + 1) % 2])
                da = dm = False
            else:
                da = dm = True
            do = False
            it = 0
            while not (da and dm and do):
                it += 1
                auxn["n"] += 1
                if auxn["n"] % 8 == 0:
                    next(auxg2, None)
                if not da:
                    da = step(ga)
                    if da and bi + 2 < len(blocks):
                        nb2 = blocks[bi + 2]
                        attn_loads(nb2[0], nb2[1], nb2[2], bufs[bi % 2])
                if not dm:
                    dm = step(gm)
                if not do and (it % 4 == 0 or (da and dm)):
                    do = step(go)
        for _ in auxg2:
            pass
        chk(4 + 10 * l)

    try:
        chk(0)
        layer(0)
        layer(1)
    except _Stop:
        pass
    P.op("sp", lambda e: e.nop(), reads=outres.rs)
    P.emit()
    return nc


def _fm(v):
    v = np.asarray(v, np.float32)
    k = v.shape[-1] // 128
    return np.ascontiguousarray(np.moveaxis(v.reshape(v.shape[:-1] + (k, 128)), -1, 0))


def _eb_tables(rpb, OWN, GR, j):
    out = np.full((5, 128, H, NSLOT, 128), NEG, np.float32)
    qc = np.arange(64)
    cs = np.clip(qc - 8, 0, 48)
    kc = np.arange(64)
    colok = (kc[:, None] >= cs[None, :]) & (kc[:, None] < cs[None, :] + 16)
    dc = np.clip(kc[:, None] - qc[None, :] + 15, 0, 30)
    own_lo, own_hi = 4, 4 + OWN // 2
    pairs = {0: None, 1: own_lo, 2: own_lo + 1, 3: own_hi - 2, 4: own_hi - 1}
    for cls in range(5):
        rel = CLS_SLOTS[cls]
        for s, r in enumerate(rel):
            for kr2 in range(2):
                for qr2 in range(2):
                    if cls == 0:
                        dr = (2 * r + kr2) - qr2
                        ok = -4 <= dr <= 3
                    else:
                        n = pairs[cls]
                        gq = OWN * j - 8 + 2 * n + qr2
                        gk = OWN * j - 8 + 2 * (n + r) + kr2
                        rs = min(max(gq - 4, 0), GR - 8)
                        ok = rs <= gk < rs + 8
                        dr = gk - gq
                    if not ok:
                        continue
                    blk = np.where(colok[None], rpb[:, dr + 7, :][:, dc], NEG)
                    out[cls, kr2 * 64:(kr2 + 1) * 64, :, s, qr2 * 64:(qr2 + 1) * 64] = np.transpose(blk, (1, 0, 2))
    return out.reshape(5, 128, H * NSLOT * 128)


def prep(inp, OWN, cores):
    x = np.asarray(inp["x"], np.float32)
    B, L, _ = x.shape
    GR = L // GW
    NJ = GR // OWN
    E = OWN + 16
    NTOK = E * GW
    p = np.arange(128)
    d = p % 64
    half = d // 32
    i32 = d % 32
    fi = i32 % 16
    freqs = (10000.0 ** (-np.arange(0, 32, 2, dtype=np.float32) / 32)).astype(np.float32)
    perm = np.zeros((128, 128), np.float32)
    partner = np.where(p % 32 < 16, p + 16, p - 16)
    perm[partner, p] = 1.0
    sign = np.where(i32 < 16, -1.0, 1.0).astype(np.float32)
    shared = {
        "w_mod": np.ascontiguousarray(inp["w_mod"], np.float32),
        "bmod": _fm(inp["b_mod"]).reshape(128, 96),
        "normg": _fm(inp["norm_g"]).reshape(128, 32),
        "w_in": np.ascontiguousarray(inp["w_in"], np.float32),
        "w_pool": np.ascontiguousarray(inp["w_pool"], np.float32),
        "pscale": _fm(inp["pool_scale"]).reshape(128, 8),
        "cdw": np.ascontiguousarray(np.transpose(_fm(inp["conv_dw"]), (0, 1, 3, 2))).reshape(128, 2 * 4 * 31),
        "cvecs": np.ascontiguousarray(np.stack([_fm(inp["conv_dw_b"]), _fm(inp["conv_ln_g"]), _fm(inp["conv_ln_b"]), _fm(inp["conv_pw_b"])], -1)).reshape(128, 32),
        "conv_pw": np.ascontiguousarray(inp["conv_pw"], np.float32),
        "w_out": np.ascontiguousarray(inp["w_out"], np.float32),
        "fng": np.asarray(inp["final_norm_g"], np.float32).reshape(1, D),
        "cmask": np.ones((128, CTX), np.float32),
        "permin": perm,
    }
    maps = []
    for (b, j) in cores:
        g = OWN * j - 8 + np.arange(E)
        valid = (g >= 0) & (g < GR)
        xl = np.zeros((E, GW, D), np.float32)
        xl[valid] = x[b].reshape(GR, GW, D)[g[valid]]
        rows = np.repeat(g, GW).astype(np.float32)
        cols = np.tile(np.arange(GW), E).astype(np.float32)
        pos = np.where(half[:, None] == 0, rows[None, :], cols[None, :]).astype(np.float32)
        ang = (pos * freqs[fi][:, None]).astype(np.float32)
        m = dict(shared)
        m["xloc"] = xl.reshape(NTOK, D)
        m["ctxin"] = np.ascontiguousarray(inp["ctx"][b], np.float32)
        m["cvec"] = np.ascontiguousarray(np.stack([_fm(inp["c"][b]), _fm(inp["c_ctx"])], -1)).reshape(128, 32)
        m["cosT"] = np.cos(ang).astype(np.float32)
        m["sinT"] = (np.sin(ang) * sign[:, None]).astype(np.float32)
        m["maskT"] = np.ascontiguousarray(np.broadcast_to(np.repeat(valid, GW).astype(np.float32)[None, :], (128, NTOK)))
        m["ebraw"] = np.stack([_eb_tables(np.asarray(inp["na_rpb"][l], np.float32), OWN, GR, j) for l in range(2)])
        maps.append(m)
    return maps


_NC_CACHE = {}


def kernel(**inputs):
    OWN = 64
    x = np.asarray(inputs["x"])
    B, L, _ = x.shape
    NJ = (L // GW) // OWN
    cores = [(b, j) for b in range(B) for j in range(NJ)]
    maps = prep(inputs, OWN, cores)
    if OWN not in _NC_CACHE:
        _NC_CACHE[OWN] = build(OWN)
    nc = _NC_CACHE[OWN]
    res = run_bass_kernel_spmd(nc, maps, core_ids=list(range(len(cores))))
    out = np.zeros((B, L, D), np.float32)
    for i, (b, j) in enumerate(cores):
        out[b, j * OWN * GW:(j + 1) * OWN * GW] = res.results[i]["out"]
    return out
```

```python
import os
import numpy as np
import concourse.bass as bass
import concourse.mybir as mybir
from concourse.bass_utils import run_bass_kernel_spmd

F32 = mybir.dt.float32
BF16 = mybir.dt.bfloat16
AF = mybir.ActivationFunctionType
ALU = mybir.AluOpType

D = 2048
PW = 6656
NCH = 52
H = 16
CTX = 256
GW = 64
NEG = -30000.0
NSLOT = 6
CLS_SLOTS = {0: [-2, -1, 0, 1, 2], 1: [-2, -1, 0, 1, 2, 3], 2: [-2, -1, 0, 1, 2],
             3: [-2, -1, 0, 1, 2], 4: [-3, -2, -1, 0, 1, 2]}


class Res:
    __slots__ = ("name", "w", "rs")

    def __init__(self, name=""):
        self.name = name
        self.w = None
        self.rs = []


class Op:
    __slots__ = ("eng", "fn", "deps", "sig", "need_sig", "chan", "ndma", "bar")


class Prog:
    ENGS = ("pe", "act", "dve", "pool", "sp")

    def __init__(self, nc):
        self.nc = nc
        self.streams = {e: [] for e in self.ENGS}
        self.ops = []
        self.last_eng = {}
        self.last_chan = {}

    def op(self, eng, fn, reads=(), writes=(), chan=None, ndma=1, extra=()):
        o = Op()
        o.eng = eng
        o.fn = fn
        o.chan = chan
        o.ndma = ndma
        o.need_sig = chan is not None
        o.bar = False
        o.sig = None
        deps = {}
        for r in reads:
            if r.w is not None:
                deps[id(r.w)] = r.w
        for w in writes:
            if w.w is not None:
                deps[id(w.w)] = w.w
            for x in w.rs:
                deps[id(x)] = x
        for x in extra:
            deps[id(x)] = x
        dl = []
        for d in deps.values():
            if d is o:
                continue
            if eng == "pe" and d.eng == "pe" and d.chan is None and chan is None:
                continue
            d.need_sig = True
            dl.append(d)
        o.deps = dl
        for r in reads:
            r.rs.append(o)
        for w in writes:
            w.w = o
            w.rs = []
        self.streams[eng].append(o)
        self.ops.append(o)
        if chan is None:
            self.last_eng[eng] = o
        else:
            self.last_chan[chan] = o
        return o

    def barrier(self):
        deps = list(self.last_eng.values()) + list(self.last_chan.values())
        keep = dict(self.last_eng)
        for eng in self.ENGS:
            o = self.op(eng, lambda e: e.nop(), extra=deps)
        self.last_eng = keep
        o.bar = True
        self.last_chan = {}

    def emit(self):
        nc = self.nc
        from contextlib import ExitStack
        es = ExitStack()
        engsem = {e: es.enter_context(nc.semaphore("s_" + e)) for e in self.ENGS}
        chansem = {}
        cnt = {e: 0 for e in self.ENGS}
        ccnt = {}
        free = {True: [], False: []}
        nslots = 0
        for o in self.ops:
            if o.bar:
                for slot in chansem.values():
                    free[slot[2]].append(slot)
                chansem = {}
            if not o.need_sig:
                continue
            if o.chan is None:
                cnt[o.eng] += 1
                o.sig = (engsem[o.eng], cnt[o.eng])
            else:
                if o.chan not in chansem:
                    sw = o.eng == "pool"
                    if free[sw]:
                        chansem[o.chan] = free[sw].pop()
                    else:
                        nslots += 1
                        chansem[o.chan] = [es.enter_context(nc.semaphore("c%d" % nslots)), 0, sw]
                slot = chansem[o.chan]
                slot[1] += 16 * o.ndma
                o.sig = (slot[0], slot[1])
        self.nsem = nslots + len(engsem)
        streams = self.streams

        def run(ename, e):
            known = {}
            for o in streams[ename]:
                need = {}
                for d in o.deps:
                    s, v = d.sig
                    k = id(s)
                    if k not in need or need[k][1] < v:
                        need[k] = (s, v)
                for k, (s, v) in need.items():
                    if known.get(k, 0) < v:
                        e.wait_ge(s, v)
                        known[k] = v
                ins = o.fn(e)
                if o.need_sig:
                    if o.chan is None:
                        ins.then_inc(o.sig[0], 1)
                    else:
                        if not isinstance(ins, (list, tuple)):
                            ins = [ins]
                        assert len(ins) == o.ndma, (len(ins), o.ndma)
                        for i_ in ins:
                            i_.then_inc(o.sig[0], 16)

        with nc.Block() as block:
            @block.tensor
            def _(e):
                run("pe", e)

            @block.scalar
            def _(e):
                run("act", e)

            @block.vector
            def _(e):
                run("dve", e)

            @block.gpsimd
            def _(e):
                run("pool", e)

            @block.sync
            def _(e):
                run("sp", e)
        es.close()


class Tl:
    _n = [0]

    def __init__(self, ap, name):
        self.ap = ap
        self.r = Res(name)
        Tl._n[0] += 1
        self.chan = "%s_%d" % (name, Tl._n[0])


class Rot:
    def __init__(self, tiles):
        self.t = tiles
        self.i = 0

    def next(self):
        t = self.t[self.i % len(self.t)]
        self.i += 1
        return t


class Arena:
    def __init__(self, nc, nbytes):
        self.t = nc.alloc_sbuf_tensor("arena", [128, nbytes // 2], BF16)
        self.nbytes = nbytes
        self.top = 0

    def alloc(self, shape, dt, name="t"):
        n = 1
        for s in shape:
            n *= s
        nb = n * (4 if dt == F32 else 2)
        off = (self.top + 63) // 64 * 64
        self.top = off + nb
        assert self.top <= self.nbytes, ("SBUF arena overflow", name, self.top)
        v = self.t[:, off // 2:(off + nb) // 2]
        if dt == F32:
            v = v.bitcast(F32)
        if len(shape) == 2:
            v = v.rearrange("p (a b) -> p a b", a=shape[0])
        elif len(shape) == 3:
            v = v.rearrange("p (a b c) -> p a b c", a=shape[0], b=shape[1])
        return Tl(v, name)


class Scr:
    def __init__(self, ap, ntok, name):
        self.ap = ap
        self.rs = [Res(name) for _ in range((ntok + 127) // 128)]

    def res(self, t0, t1):
        return self.rs[t0 // 128:(t1 + 127) // 128]


class _Stop(Exception):
    pass


def build(OWN, stop=99):
    def chk(k):
        if stop == k:
            raise _Stop()
    E = OWN + 16
    NTOK = E * GW
    NP = E // 2
    nc = bass.Bass("TRN2", target_bir_lowering=False)
    P = Prog(nc)

    def din(name, shape, dt=F32):
        return nc.dram_tensor(name, list(shape), dt, kind="ExternalInput").ap()

    def dscr(name, shape, dt=BF16):
        return nc.dram_tensor(name, list(shape), dt, kind="Internal").ap()

    xloc = din("xloc", [NTOK, D])
    ctxin = din("ctxin", [CTX, D])
    cvec = din("cvec", [128, 32])
    w_mod = din("w_mod", [2, D, 3 * D])
    bmod = din("bmod", [128, 2 * 48])
    normg = din("normg", [128, 2 * 16])
    w_in = din("w_in", [2, D, PW])
    w_pool = din("w_pool", [2, 4, 128, 128])
    pscale = din("pscale", [128, 8])
    cdw = din("cdw", [128, 2 * 4 * 31])
    cvecs = din("cvecs", [128, 2 * 4 * 4])
    conv_pw = din("conv_pw", [2, 512, 512])
    w_out = din("w_out", [2, D, D])
    fng = din("fng", [1, D])
    cosT = din("cosT", [128, NTOK])
    sinT = din("sinT", [128, NTOK])
    maskT = din("maskT", [128, NTOK])
    cmask = din("cmask", [128, CTX])
    ebraw = din("ebraw", [2, 5, 128, H * NSLOT * 128])
    permin = din("permin", [128, 128])
    out = nc.dram_tensor("out", [OWN * GW, D], F32, kind="ExternalOutput").ap()

    wbf = [dscr("wbf%d" % l, [D, PW]) for l in range(2)]
    woutbf = [dscr("woutbf%d" % l, [16, 128, 2048]) for l in range(2)]
    ebs = [dscr("ebs%d" % l, [5, 128, H * NSLOT * 128]) for l in range(2)]
    modrow = dscr("modrow", [4, D], F32)
    wres = {k: Res(k) for k in ("wbf0", "wbf1", "wout0", "wout1", "ebs0", "ebs1", "modrow")}

    class Seg:
        pass

    def mkseg(name, ntok, xsrc, rope, midx):
        s = Seg()
        s.name = name
        s.ntok = ntok
        s.rope = rope
        s.midx = midx
        s.x = [Scr(xsrc, ntok, name + "x0"), Scr(dscr(name + "_x1", [ntok, D], F32), ntok, name + "x1")]
        s.qT = Scr(dscr(name + "_qT", [8, 128, ntok]), ntok, name + "qT")
        s.kT = Scr(dscr(name + "_kT", [8, 128, ntok]), ntok, name + "kT")
        s.v = Scr(dscr(name + "_v", [ntok, H * 65]), ntok, name + "v")
        s.gate = Scr(dscr(name + "_g", [16, 128, ntok]), ntok, name + "g")
        s.uT = Scr(dscr(name + "_uT", [4, 128, ntok]), ntok, name + "uT")
        s.yT = Scr(dscr(name + "_yT", [4, 128, ntok]), ntok, name + "yT")
        return s

    lat = mkseg("lat", NTOK, xloc, True, 0)
    cseg = mkseg("ctx", CTX, ctxin, False, 1)
    lat.mask = maskT
    cseg.mask = cmask
    outres = Scr(out, OWN * GW, "out")

    ar = Arena(nc, 207 * 1024)
    A = ar.alloc

    identf = A([128], F32, "identf")
    ident = A([128], BF16, "ident")
    perm = A([128], BF16, "perm")
    onesm = A([128], F32, "onesm")
    eps6 = A([1], F32, "eps6")
    eps5 = A([1], F32, "eps5")
    cact = A([16, 2], F32, "cact")
    bmod_s = A([2, 48], F32, "bmod")
    normg_s = A([2, 16], F32, "normg")
    pscale_s = A([2, 4], F32, "pscale")
    cdw_s = A([2, 4, 31], F32, "cdw")
    cvecs_s = A([2, 4, 4], F32, "cvecs")
    modv = A([2, 48, 2], F32, "modv")
    Amod = A([2, 2, 16], F32, "Amod")
    wpool_s = A([4, 128], BF16, "wpool")
    cpw_s = A([4, 512], BF16, "cpw")
    ctxK = A([8, CTX], BF16, "ctxK")
    ctxV = A([2, H, 65], BF16, "ctxV")
    persist_top = ar.top

    psA = [Tl(nc.alloc_psum_tensor("psA%d" % i, [128, 512], F32)[:], "psA") for i in range(3)]
    psB = [Tl(nc.alloc_psum_tensor("psB%d" % i, [128, 512], F32)[:], "psB") for i in range(1)]
    psS = [Tl(nc.alloc_psum_tensor("psS%d" % i, [128, 1024], F32)[:], "psS") for i in range(2)]
    psrot = Rot(psA)

    def dma_in(dst, dst_ap, src_ap, src_res, eng="sp"):
        return P.op(eng, lambda e: e.dma_start(out=dst_ap, in_=src_ap), reads=src_res, writes=[dst.r], chan=dst.chan)

    def dma_out(src, src_ap, dst_ap, dst_res, eng="pool"):
        return P.op(eng, lambda e: e.dma_start(out=dst_ap, in_=src_ap), reads=[src.r], writes=dst_res, chan=src.chan + "o")

    P.op("pool", lambda e: e.memset(identf.ap, 1.0), writes=[identf.r])
    P.op("pool", lambda e: e.affine_select(out=identf.ap, in_=identf.ap, pattern=[[-1, 128]], compare_op=ALU.is_equal,
                                          fill=0.0, base=0, channel_multiplier=1), reads=[identf.r], writes=[identf.r])
    P.op("dve", lambda e: e.tensor_copy(out=ident.ap, in_=identf.ap), reads=[identf.r], writes=[ident.r])
    P.op("pool", lambda e: e.memset(onesm.ap, 1.0 / 512), writes=[onesm.r])
    P.op("pool", lambda e: e.memset(eps6.ap, 1e-6), writes=[eps6.r])
    P.op("pool", lambda e: e.memset(eps5.ap, 1e-5), writes=[eps5.r])

    ph0 = ar.top
    permf = A([128], F32, "permf")
    cvec_s = A([16, 2], F32, "cvec")
    dma_in(permf, permf.ap, permin, [])
    P.op("dve", lambda e: e.tensor_copy(out=perm.ap, in_=permf.ap), reads=[permf.r], writes=[perm.r])
    dma_in(cvec_s, cvec_s.ap, cvec.rearrange("p (a b) -> p a b", a=16), [])
    dma_in(bmod_s, bmod_s.ap, bmod.rearrange("p (a b) -> p a b", a=2), [])
    dma_in(normg_s, normg_s.ap, normg.rearrange("p (a b) -> p a b", a=2), [])
    dma_in(pscale_s, pscale_s.ap, pscale.rearrange("p (a b) -> p a b", a=2), [])
    dma_in(cdw_s, cdw_s.ap, cdw.rearrange("p (a b c) -> p a b c", a=2, b=4), [])
    dma_in(cvecs_s, cvecs_s.ap, cvecs.rearrange("p (a b c) -> p a b c", a=2, b=4), [])
    P.op("act", lambda e: e.activation(out=cact.ap, in_=cvec_s.ap, func=AF.Silu), reads=[cvec_s.r], writes=[cact.r])

    wm = Rot([A([3 * D], F32, "wm") for _ in range(2)])
    psM = psA[0]
    def mod_gen():
        for l in range(2):
            for kc in range(16):
                w = wm.next()
                dma_in(w, w.ap, w_mod[l, kc * 128:(kc + 1) * 128, :], [])

                def mm(e, w=w, kc=kc):
                    for fo in range(48):
                        i = e.matmul(psM.ap[:, 2 * fo:2 * fo + 2], lhsT=w.ap[:, fo * 128:(fo + 1) * 128], rhs=cact.ap[:, kc, :],
                                     start=True, stop=True)
                    return i
                P.op("pe", mm, reads=[w.r, cact.r], writes=[psM.r])
                mv = modv.ap[:, l].rearrange("p a b -> p (a b)")
                if kc == 0:
                    P.op("dve", lambda e, mv=mv: e.tensor_copy(out=mv, in_=psM.ap[:, 0:96]), reads=[psM.r], writes=[modv.r])
                else:
                    P.op("dve", lambda e, mv=mv: e.tensor_tensor(out=mv, in0=psM.ap[:, 0:96], in1=mv, op=ALU.add),
                         reads=[psM.r, modv.r], writes=[modv.r])
                yield
            for m in range(2):
                P.op("dve", lambda e, l=l, m=m: e.tensor_tensor(out=modv.ap[:, l, :, m], in0=modv.ap[:, l, :, m], in1=bmod_s.ap[:, l, :], op=ALU.add),
                     reads=[modv.r, bmod_s.r], writes=[modv.r])
                P.op("dve", lambda e, l=l, m=m: e.tensor_scalar(out=Amod.ap[:, l, m, :], in0=modv.ap[:, l, 16:32, m], scalar1=1.0, scalar2=None, op0=ALU.add),
                     reads=[modv.r], writes=[Amod.r])
                P.op("dve", lambda e, l=l, m=m: e.tensor_tensor(out=Amod.ap[:, l, m, :], in0=Amod.ap[:, l, m, :], in1=normg_s.ap[:, l, :], op=ALU.mult),
                     reads=[Amod.r, normg_s.r], writes=[Amod.r])
                P.op("pool", lambda e, l=l, m=m: e.dma_start(out=modrow[2 * l + m].rearrange("(k p) -> p k", p=128), in_=modv.ap[:, l, 32:48, m],
                                                          allow_slow_non_contiguous=True),
                     reads=[modv.r], writes=[wres["modrow"]], chan="modrow")


    def cast_win(l, wf, wb):
        engs = ["dve", "act", "pool"]
        wk = "wbf%d" % l
        for kc in range(16):
            f_ = wf.next()
            b_ = wb.next()
            dma_in(f_, f_.ap, w_in[l, kc * 128:(kc + 1) * 128, :], [])
            for part in range(4):
                eg = engs[(kc * 4 + part) % 3]
                sl = slice(part * 1664, (part + 1) * 1664)
                if eg == "act":
                    P.op("act", lambda e, f_=f_, b_=b_, sl=sl: e.activation(out=b_.ap[:, sl], in_=f_.ap[:, sl], func=AF.Copy),
                         reads=[f_.r], writes=[b_.r])
                else:
                    P.op(eg, lambda e, f_=f_, b_=b_, sl=sl: e.tensor_copy(out=b_.ap[:, sl], in_=f_.ap[:, sl]), reads=[f_.r], writes=[b_.r])
            P.op("sp", lambda e, b_=b_, kc=kc: e.dma_start(out=wbf[l][kc * 128:(kc + 1) * 128, :], in_=b_.ap),
                 reads=[b_.r], writes=[wres[wk]], chan=b_.chan + "s")
            yield

    def interleave2(g1, g2):
        d1 = d2 = False
        while not (d1 and d2):
            if not d1:
                try:
                    next(g1)
                except StopIteration:
                    d1 = True
            if not d2:
                try:
                    next(g2)
                except StopIteration:
                    d2 = True

    wf0 = Rot([A([PW], F32, "wf") for _ in range(2)])
    wb0 = Rot([A([PW], BF16, "wb") for _ in range(2)])
    interleave2(mod_gen(), cast_win(0, wf0, wb0))

    def norm_tile(seg, l, tok0, hT, col0, pools, hres):
        xt = pools["xt"].next()
        junk = pools["junk"]
        xn = pools["xn"].next()
        ss = pools["ss"].next()
        pT = pools["pT"].next()
        xs = seg.x[l]
        dma_in(xt, xt.ap, xs.ap[tok0:tok0 + 128, :], xs.res(tok0, tok0 + 128))
        P.op("pool", lambda e: e.memset(ss.ap, 0.0), writes=[ss.r])
        P.op("act", lambda e: e.activation(out=junk.ap, in_=xt.ap, func=AF.Square, accum_out=ss.ap[:, 0:1]),
             reads=[xt.r, ss.r], writes=[ss.r, junk.r])
        NT = int(os.environ.get('NT', '9'))
        if NT < 2:
            return
        P.op("act", lambda e: e.activation(out=ss.ap[:, 1:2], in_=ss.ap[:, 0:1], func=AF.Ln, scale=1.0 / D, bias=eps6.ap[:, 0:1]),
             reads=[ss.r, eps6.r], writes=[ss.r])
        P.op("act", lambda e: e.activation(out=ss.ap[:, 2:3], in_=ss.ap[:, 1:2], func=AF.Exp, scale=-0.5), reads=[ss.r], writes=[ss.r])
        if NT < 3:
            return
        P.op("dve", lambda e: e.tensor_scalar(out=xn.ap, in0=xt.ap, scalar1=ss.ap[:, 2:3], scalar2=None, op0=ALU.mult),
             reads=[xt.r, ss.r], writes=[xn.r])
        if NT < 4:
            return
        pTv = pT.ap.bitcast(BF16).rearrange("p (a b) -> p a b", b=128)

        def tr(e):
            for k in range(16):
                i = e.transpose(out=pTv[:, k, :], in_=xn.ap[:, k * 128:(k + 1) * 128], identity=ident.ap)
            return i
        P.op("pe", tr, reads=[xn.r, ident.r], writes=[pT.r])
        if NT < 5:
            return
        m = seg.midx
        for k in range(16):
            dst = hT.ap[:, k, col0:col0 + 128]
            EV = os.environ.get('EV', 'act')
            if (k % 2 == 0 and EV != 'dve') or EV == 'act':
                P.op("act", lambda e, k=k, dst=dst: e.activation(out=dst, in_=pTv[:, k, :], func=AF.Identity,
                                                                 scale=Amod.ap[:, l, m, k:k + 1], bias=modv.ap[:, l, k, m:m + 1]),
                     reads=[pT.r, Amod.r, modv.r], writes=[hres[k % 2]])
            else:
                P.op("dve", lambda e, k=k, dst=dst: e.tensor_scalar(out=dst, in0=pTv[:, k, :], scalar1=Amod.ap[:, l, m, k:k + 1],
                                                                    scalar2=modv.ap[:, l, k, m:m + 1], op0=ALU.mult, op1=ALU.add),
                     reads=[pT.r, Amod.r, modv.r], writes=[hres[k % 2]])

    def layer(l):
        P.barrier()
        ar.top = persist_top
        wf = Rot([A([2048], F32, "wf") for _ in range(2)])
        f_ = wf.next()
        dma_in(f_, f_.ap[:, 0:512].rearrange("p (g d) -> p g d", g=4), w_pool[l].rearrange("g c d -> c g d"), [])
        P.op("dve", lambda e, f_=f_: e.tensor_copy(out=wpool_s.ap.rearrange("p g d -> p (g d)"), in_=f_.ap[:, 0:512]), reads=[f_.r], writes=[wpool_s.r])
        f_ = wf.next()
        dma_in(f_, f_.ap[:, 0:2048].rearrange("p (a d) -> p a d", a=4), conv_pw[l].rearrange("(a c) d -> c a d", c=128), [])
        P.op("dve", lambda e, f_=f_: e.tensor_copy(out=cpw_s.ap.rearrange("p a d -> p (a d)"), in_=f_.ap[:, 0:2048]), reads=[f_.r], writes=[cpw_s.r])
        chk(1 + 10 * l)
        P.barrier()
        ar.top = persist_top
        pB = {
            "xt": Rot([A([D], F32, "xt") for _ in range(2)]),
            "junk": A([D], BF16, "junk"),
            "xn": Rot([A([D], BF16, "xn") for _ in range(2)]),
            "ss": Rot([A([4], F32, "ss") for _ in range(2)]),
            "pT": Rot(psS),
        }
        hTs = [A([16, 512], BF16, "hT") for _ in range(2)]
        hTress = [[[Res("hT") for _ in range(2)] for _ in range(4)] for _ in range(2)]
        wts = Rot([A([16, 512], BF16, "wt") for _ in range(2)])
        wstate = {"g": -1, "wt": None}
        wv = A([16, 1024], BF16, "wv")
        cos_s = A([512], F32, "cos")
        sin_s = A([512], F32, "sin")
        msk_s = A([512], F32, "msk")
        stg = Rot([A([512], BF16, "stg") for _ in range(4)])
        qbs = Rot([A([512], BF16, "qb") for _ in range(2)])
        t1s = Rot([A([512], F32, "t1") for _ in range(2)])
        t2s = Rot([A([512], F32, "t2") for _ in range(2)])
        sg = A([4, 512], F32, "sg")
        vst = Rot([A([8, 65], BF16, "vst") for _ in range(2)])
        for v_ in vst.t:
            P.op("pool", lambda e, v_=v_: e.memset(v_.ap[:, :, 64:65], 1.0), writes=[v_.r])

        auxf = A([3072], F32, "auxf")
        auxb = A([3072], BF16, "auxb")

        def aux_casts():
            for fc in range(16):
                dma_in(auxf, auxf.ap[:, 0:D], w_out[l, fc * 128:(fc + 1) * 128, :], [])
                P.op("pool", lambda e: e.tensor_copy(out=auxb.ap[:, 0:D], in_=auxf.ap[:, 0:D]), reads=[auxf.r], writes=[auxb.r])
                P.op("pool", lambda e, fc=fc: e.dma_start(out=woutbf[l][fc], in_=auxb.ap[:, 0:D]), reads=[auxb.r], writes=[wres["wout%d" % l]], chan=auxb.chan + "o")
                yield
            n_ = 4 * NSLOT * 128
            for c in range(5):
                for q4 in range(4):
                    dma_in(auxf, auxf.ap[:, 0:n_], ebraw[l, c, :, q4 * n_:(q4 + 1) * n_], [])
                    P.op("act", lambda e: e.activation(out=auxb.ap[:, 0:n_], in_=auxf.ap[:, 0:n_], func=AF.Exp), reads=[auxf.r], writes=[auxb.r])
                    P.op("pool", lambda e, c=c, q4=q4: e.dma_start(out=ebs[l][c, :, q4 * n_:(q4 + 1) * n_], in_=auxb.ap[:, 0:n_]),
                         reads=[auxb.r], writes=[wres["ebs%d" % l]], chan=auxb.chan + "o")
                    yield
        auxg = aux_casts()
        auxc = {"n": 0}

        def normB(seg, tok0, T, hb):
            for ti in range(T // 128):
                norm_tile(seg, l, tok0 + ti * 128, hTs[hb], ti * 128, pB, hTress[hb][ti])

        def passB(seg, tok0, T, kinds, hb, pre_v=None):
            nt = T // 128
            wstate["g"] = -1
            hT = hTs[hb]
            hTall = [r_ for rr in hTress[hb] for r_ in rr]
            if seg.rope:
                dma_in(cos_s, cos_s.ap[:, 0:T], cosT[:, tok0:tok0 + T], [])
                dma_in(sin_s, sin_s.ap[:, 0:T], sinT[:, tok0:tok0 + T], [])
            dma_in(msk_s, msk_s.ap[:, 0:T], seg.mask[:, tok0:tok0 + T], [])
            order = []
            if "u" in kinds:
                order += [(f, "u", f) for f in range(0, 4)]
            if "g" in kinds:
                order += [(f, "g", f - 4) for f in range(4, 8)]
            if "q" in kinds:
                order += [(f, "q", f - 8) for f in range(8, 16)]
            if "k" in kinds:
                order += [(f, "k", f - 16) for f in range(16, 24)]
            if "g" in kinds:
                order += [(f, "g", f - 32 + 4) for f in range(32, 40)]
            if "y" in kinds:
                order += [(f, "sg", f - 44) for f in range(44, 48)]
                order += [(f, "y", f - 40) for f in range(40, 44)]
            if "g" in kinds:
                order += [(f, "g", f - 48 + 12) for f in range(48, 52)]
            pend = []

            def flush():
                while pend:
                    pend.pop(0)()
            for (f, kind, idx) in order:
                if wstate["g"] != f // 4:
                    wstate["g"] = f // 4
                    wstate["wt"] = wts.next()
                    g0 = (f // 4) * 512
                    dma_in(wstate["wt"], wstate["wt"].ap, wbf[l][:, g0:g0 + 512].rearrange("(k p) c -> p k c", p=128), [wres["wbf%d" % l]])
                wt = wstate["wt"]
                fo = (f % 4) * 128
                ps = psrot.next()

                def mm(e, wt=wt, ps=ps, fo=fo):
                    for kc in range(16):
                        i = e.matmul(ps.ap[:, 0:T], lhsT=wt.ap[:, kc, fo:fo + 128], rhs=hT.ap[:, kc, 0:T], start=(kc == 0), stop=(kc == 15))
                    return i
                P.op("pe", mm, reads=[wt.r] + hTall, writes=[ps.r])
                flush()
                auxc["n"] += 1
                if auxc["n"] % 8 == 0:
                    next(auxg, None)
                if kind == "u":
                    st = stg.next()
                    P.op("dve", lambda e, st=st, ps=ps: e.tensor_tensor(out=st.ap[:, 0:T], in0=ps.ap[:, 0:T], in1=msk_s.ap[:, 0:T], op=ALU.mult),
                         reads=[ps.r, msk_s.r], writes=[st.r])
                    dma_out(st, st.ap[:, 0:T], seg.uT.ap[idx, :, tok0:tok0 + T], seg.uT.res(tok0, tok0 + T))
                elif kind == "g":
                    st = stg.next()
                    P.op("act", lambda e, st=st, ps=ps: e.activation(out=st.ap[:, 0:T], in_=ps.ap[:, 0:T], func=AF.Silu), reads=[ps.r], writes=[st.r])
                    dma_out(st, st.ap[:, 0:T], seg.gate.ap[idx, :, tok0:tok0 + T], seg.gate.res(tok0, tok0 + T))
                elif kind in ("q", "k"):
                    dstS = seg.qT if kind == "q" else seg.kT
                    st = stg.next()
                    if not seg.rope:
                        P.op("act", lambda e, st=st, ps=ps: e.activation(out=st.ap[:, 0:T], in_=ps.ap[:, 0:T], func=AF.Copy), reads=[ps.r], writes=[st.r])
                    else:
                        qb = qbs.next()
                        t1 = t1s.next()
                        t2 = t2s.next()
                        pr = psB[0]
                        P.op("act", lambda e, qb=qb, ps=ps: e.activation(out=qb.ap[:, 0:T], in_=ps.ap[:, 0:T], func=AF.Copy), reads=[ps.r], writes=[qb.r])
                        P.op("dve", lambda e, t1=t1, ps=ps: e.tensor_tensor(out=t1.ap[:, 0:T], in0=ps.ap[:, 0:T], in1=cos_s.ap[:, 0:T], op=ALU.mult),
                             reads=[ps.r, cos_s.r, qb.r], writes=[t1.r])

                        def rope_rest(qb=qb, t1=t1, t2=t2, pr=pr, st=st, dstS=dstS, idx=idx):
                            P.op("pe", lambda e: e.matmul(pr.ap[:, 0:T], lhsT=perm.ap, rhs=qb.ap[:, 0:T], start=True, stop=True),
                                 reads=[perm.r, qb.r], writes=[pr.r])
                            P.op("dve", lambda e: e.tensor_tensor(out=t2.ap[:, 0:T], in0=pr.ap[:, 0:T], in1=sin_s.ap[:, 0:T], op=ALU.mult),
                                 reads=[pr.r, sin_s.r], writes=[t2.r])
                            P.op("dve", lambda e: e.tensor_tensor(out=st.ap[:, 0:T], in0=t1.ap[:, 0:T], in1=t2.ap[:, 0:T], op=ALU.add),
                                 reads=[t1.r, t2.r], writes=[st.r])
                            dma_out(st, st.ap[:, 0:T], dstS.ap[idx, :, tok0:tok0 + T], dstS.res(tok0, tok0 + T))
                        pend.append(rope_rest)
                        continue
                    dma_out(st, st.ap[:, 0:T], dstS.ap[idx, :, tok0:tok0 + T], dstS.res(tok0, tok0 + T))
                elif kind == "sg":
                    P.op("act", lambda e, ps=ps, idx=idx: e.activation(out=sg.ap[:, idx, 0:T], in_=ps.ap[:, 0:T], func=AF.Sigmoid), reads=[ps.r], writes=[sg.r])
                    P.op("pool", lambda e, idx=idx: e.tensor_tensor(out=sg.ap[:, idx, 0:T], in0=sg.ap[:, idx, 0:T], in1=msk_s.ap[:, 0:T], op=ALU.mult),
                         reads=[sg.r, msk_s.r], writes=[sg.r])
                elif kind == "y":
                    st = stg.next()
                    P.op("dve", lambda e, st=st, ps=ps, idx=idx: e.tensor_tensor(out=st.ap[:, 0:T], in0=ps.ap[:, 0:T], in1=sg.ap[:, idx, 0:T], op=ALU.mult),
                         reads=[ps.r, sg.r], writes=[st.r])
                    dma_out(st, st.ap[:, 0:T], seg.yT.ap[idx, :, tok0:tok0 + T], seg.yT.res(tok0, tok0 + T))
            flush()
            if pre_v is not None:
                pre_v()
            if "v" in kinds:
                dma_in(wv, wv.ap, wbf[l][:, 3072:4096].rearrange("(k p) c -> p k c", p=128), [wres["wbf%d" % l]])
                for ti in range(nt):
                    for half in range(2):
                        ps = psrot.next()

                        def mmv(e, ps=ps, ti=ti, half=half):
                            for kc in range(16):
                                i = e.matmul(ps.ap, lhsT=hT.ap[:, kc, ti * 128:(ti + 1) * 128],
                                             rhs=wv.ap[:, kc, half * 512:(half + 1) * 512], start=(kc == 0), stop=(kc == 15))
                            return i
                        P.op("pe", mmv, reads=[wv.r] + hTall, writes=[ps.r])
                        vs = vst.next()
                        if half == 0:
                            P.op("act", lambda e, vs=vs, ps=ps: e.activation(out=vs.ap[:, :, 0:64], in_=ps.ap.rearrange("p (h d) -> p h d", h=8), func=AF.Copy), reads=[ps.r], writes=[vs.r])
                        else:
                            P.op("dve", lambda e, vs=vs, ps=ps: e.tensor_copy(out=vs.ap[:, :, 0:64], in_=ps.ap.rearrange("p (h d) -> p h d", h=8)), reads=[ps.r], writes=[vs.r])
                        t0 = tok0 + ti * 128
                        dma_out(vs, vs.ap, seg.v.ap[t0:t0 + 128, half * 520:(half + 1) * 520].rearrange("t (h d) -> t h d", h=8), seg.v.res(t0, t0 + 128))

        chk(8)
        ALLK = ("u", "g", "q", "k", "v", "y")
        HALO = ("u", "k", "v", "y")
        if l == 0:
            r_lo, r_hi = 4, OWN + 12
        else:
            r_lo, r_hi = 8, OWN + 8
        blocksB = [(cseg, 0, CTX, ALLK if l == 0 else ("k", "v")), (lat, (r_lo - 4) * GW, 4 * GW, HALO)]
        blocksB += [(lat, r * GW, 8 * GW, ALLK) for r in range(r_lo, r_hi, 8)]
        blocksB += [(lat, r_hi * GW, 4 * GW, HALO)]
        normB(blocksB[0][0], blocksB[0][1], blocksB[0][2], 0)
        for bi, (sg_, t0_, T_, kn_) in enumerate(blocksB):
            if bi + 1 < len(blocksB):
                nb = blocksB[bi + 1]
                pre = (lambda nb=nb, bi=bi: normB(nb[0], nb[1], nb[2], (bi + 1) % 2))
            else:
                pre = None
            passB(sg_, t0_, T_, kn_, bi % 2, pre)
            if bi == 0:
                dma_in(ctxK, ctxK.ap, cseg.kT.ap.rearrange("c p t -> p c t"), cseg.kT.res(0, CTX))
                for a in range(2):
                    P.op("sp", lambda e, a=a: e.dma_start(out=ctxV.ap[:, a], in_=cseg.v.ap[a * 128:(a + 1) * 128, :].rearrange("t (h d) -> t h d", h=H)),
                         reads=cseg.v.res(a * 128, (a + 1) * 128), writes=[ctxV.r], chan=ctxV.chan)
        for _ in auxg:
            pass
        chk(3 + 10 * l)
        P.barrier()
        ar.top = persist_top
        TB = 256
        Kw = A([8, 8 * 128], BF16, "Kw")
        Vw = A([8, H, 65], BF16, "Vw")
        ebsp = Rot([A([2, NSLOT * 128], BF16, "ebsp") for _ in range(3)])
        Pts = Rot([A([8 * 128], BF16, "Pt") for _ in range(2)])
        nas = Rot([A([1024], BF16, "na") for _ in range(2)])
        rdn = Rot([A([4], F32, "rdn") for _ in range(4)])
        gtsAC = A([8, TB], BF16, "gtsAC")
        bufs = [dict(mixT=A([16, TB], BF16, "mixT"), gts=gtsAC, gB=A([8, TB], BF16, "gB"), Qb=A([8, TB], BF16, "Qb"),
                     mixA=Res("mixA"), mixB=Res("mixB"), mixC=Res("mixC")) for _ in range(2)]
        ubuf = A([4, TB + 16], BF16, "ubuf")
        mbuf = A([TB + 16], F32, "mbuf")
        tmpA = [A([TB + 16], F32, "tmpA%d" % i) for i in range(4)]
        icn = A([4, TB], F32, "icn")
        pooled = A([4, TB], BF16, "pooled")
        ybuf = A([4, TB + 30], BF16, "ybuf")
        cacc = A([4, TB], F32, "cacc")
        caccr = [Res("cacc%d" % c) for c in range(4)]
        csq = A([TB], F32, "csq")
        cmean = A([TB], F32, "cmean")
        cvar = A([TB], F32, "cvar")
        cd = A([TB], F32, "cd")
        cz = A([4, TB], BF16, "cz")
        xin = Rot([A([512], F32, "xin") for _ in range(2)])
        xnew = Rot([A([D], F32, "xnew") for _ in range(2)])
        gbc = A([D], F32, "gbc")
        fbc = A([D], F32, "fbc") if l == 1 else None
        junkC = A([D], BF16, "junkC")
        ssC = Rot([A([4], F32, "ssC") for _ in range(2)])
        psOr = Rot([psB[0], psA[2]])
        psG = Rot([psA[0], psA[1]])
        psSr = Rot(psS)
        if l == 1:
            P.op("sp", lambda e: e.dma_start(out=fbc.ap, in_=fng.broadcast_to([128, D])), writes=[fbc.r], chan=fbc.chan)

        def halo_load(dst, scr, tok0, T, hl, ntok, is_mask=False):
            lo = max(tok0 - hl, 0)
            hi = min(tok0 + T + hl, ntok)
            a = lo - (tok0 - hl)
            b = a + (hi - lo)
            W = T + 2 * hl
            if is_mask:
                if a > 0:
                    P.op("pool", lambda e: e.memset(dst.ap[:, 0:a], 0.0), writes=[dst.r])
                if b < W:
                    P.op("pool", lambda e: e.memset(dst.ap[:, b:W], 0.0), writes=[dst.r])
                dma_in(dst, dst.ap[:, a:b], scr[:, lo:hi], [])
            else:
                if a > 0:
                    P.op("pool", lambda e: e.memset(dst.ap[:, :, 0:a], 0.0), writes=[dst.r])
                if b < W:
                    P.op("pool", lambda e: e.memset(dst.ap[:, :, b:W], 0.0), writes=[dst.r])
                dma_in(dst, dst.ap[:, :, a:b], scr.ap[:, :, lo:hi].rearrange("c p t -> p c t"), scr.res(lo, hi))

        def attn_loads(seg, tok0, T, bf):
            gB, Qb = bf["gB"], bf["Qb"]
            npair = T // 128
            n0 = tok0 // 128
            dma_in(gB, gB.ap[:, :, 0:T], seg.gate.ap[4:12, :, tok0:tok0 + T].rearrange("c p t -> p c t"), seg.gate.res(tok0, tok0 + T))
            dma_in(Qb, Qb.ap[:, :, 0:T], seg.qT.ap[:, :, tok0:tok0 + T].rearrange("c p t -> p c t"), seg.qT.res(tok0, tok0 + T))
            if seg is lat:
                kp_lo = max(n0 - 3, 0)
                kp_hi = min(n0 + npair + 3, NP)
                nk = kp_hi - kp_lo
                assert nk <= 8
                dma_in(Kw, Kw.ap[:, :, 0:nk * 128], seg.kT.ap[:, :, kp_lo * 128:kp_hi * 128].rearrange("c p t -> p c t"),
                       seg.kT.res(kp_lo * 128, kp_hi * 128))
                dma_in(Vw, Vw.ap[:, 0:nk], seg.v.ap[kp_lo * 128:kp_hi * 128, :].rearrange("(s t) (h d) -> t s h d", t=128, h=H),
                       seg.v.res(kp_lo * 128, kp_hi * 128))

        def attention(seg, tok0, T, bf):
            mixT, gB, Qb = bf["mixT"], bf["gB"], bf["Qb"]
            npair = T // 128
            n0 = tok0 // 128
            is_lat = seg is lat
            kp_lo = max(n0 - 3, 0) if is_lat else 0
            own_lo = 4
            own_hi = 4 + OWN // 2
            for pi in range(npair):
                n = n0 + pi
                cls = 0
                if is_lat:
                    if n == own_lo:
                        cls = 1
                    elif n == own_lo + 1:
                        cls = 2
                    elif n == own_hi - 2:
                        cls = 3
                    elif n == own_hi - 1:
                        cls = 4
                    rel = CLS_SLOTS[cls]
                    nloc = len(rel)
                else:
                    rel = []
                    nloc = 0
                nsl = nloc + 2
                na = nas.next()
                state = {}

                def qk(h, pi=pi, rel=rel, nloc=nloc, nsl=nsl, n=n, cls=cls, state=state):
                    S = psSr.next()
                    state[h] = S
                    ch, pb = h // 2, (h % 2) * 64
                    if nloc and h % 2 == 0:
                        ebt = ebsp.next()
                        state["ebt", h // 2] = ebt
                        off = h * NSLOT * 128
                        dma_in(ebt, ebt.ap, ebs[l][cls, :, off:off + 2 * NSLOT * 128].rearrange("p (a b) -> p a b", a=2), [wres["ebs%d" % l]])

                    def f(e):
                        for s in range(nsl):
                            if s < nloc:
                                ko = (n + rel[s] - kp_lo) * 128
                                lk = Kw.ap[pb:pb + 64, ch, ko:ko + 128]
                            else:
                                lk = ctxK.ap[pb:pb + 64, ch, (s - nloc) * 128:(s - nloc + 1) * 128]
                            i = e.matmul(S.ap[:, s * 128:(s + 1) * 128], lhsT=lk, rhs=Qb.ap[pb:pb + 64, ch, pi * 128:(pi + 1) * 128],
                                         start=True, stop=True)
                        return i
                    P.op("pe", f, reads=[Kw.r, ctxK.r, Qb.r] if is_lat else [ctxK.r, Qb.r], writes=[S.r])

                def rest(h, pi=pi, rel=rel, nloc=nloc, nsl=nsl, n=n, cls=cls, na=na, state=state):
                    S = state.pop(h)
                    Pt = Pts.next()
                    n1 = min(nsl, 4) * 128
                    P.op("act", lambda e: e.activation(out=Pt.ap[:, 0:n1], in_=S.ap[:, 0:n1], func=AF.Exp, scale=0.125), reads=[S.r], writes=[Pt.r])
                    if nsl > 4:
                        P.op("act", lambda e: e.activation(out=Pt.ap[:, n1:nsl * 128], in_=S.ap[:, n1:nsl * 128], func=AF.Exp, scale=0.125),
                             reads=[S.r], writes=[Pt.r])
                    if nloc:
                        ebt = state["ebt", h // 2]
                        eba = ebt.ap[:, h % 2, 0:nloc * 128]
                        P.op("dve", lambda e: e.tensor_tensor(out=Pt.ap[:, 0:nloc * 128], in0=Pt.ap[:, 0:nloc * 128], in1=eba, op=ALU.mult),
                             reads=[Pt.r, ebt.r], writes=[Pt.r])
                    if h % 4 == 0:
                        state["O"] = psOr.next()
                    O_ = state["O"]
                    Ov = O_.ap.rearrange("p (a b) -> p a b", a=4)
                    oreg = Ov[:, h % 4, 0:65]

                    def pv(e):
                        for s in range(nsl):
                            if s < nloc:
                                rv = Vw.ap[:, n + rel[s] - kp_lo, h, :]
                            else:
                                rv = ctxV.ap[:, s - nloc, h, :]
                            i = e.matmul(oreg, lhsT=Pt.ap[:, s * 128:(s + 1) * 128], rhs=rv, start=(s == 0), stop=(s == nsl - 1))
                        return i
                    P.op("pe", pv, reads=[Pt.r, ctxV.r] + ([Vw.r] if is_lat else []), writes=[O_.r])
                    if h % 4 == 3:
                        rd = rdn.next()
                        g4 = h // 4
                        P.op("dve", lambda e: e.reciprocal(out=rd.ap[:, 0:4], in_=Ov[:, :, 64]), reads=[O_.r], writes=[rd.r])
                        P.op("dve", lambda e: e.tensor_tensor(out=na.ap[:, g4 * 256:(g4 + 1) * 256].rearrange("p (a b) -> p a b", a=4), in0=Ov[:, :, 0:64],
                                                              in1=rd.ap[:, 0:4].unsqueeze(2).broadcast_to([128, 4, 64]), op=ALU.mult),
                             reads=[O_.r, rd.r], writes=[na.r])
                qk(0)
                for h in range(H):
                    if h + 1 < H:
                        qk(h + 1)
                    rest(h)
                    yield
                psT = psG.next()
                pTv = psT.ap.bitcast(BF16).rearrange("p (a b) -> p a b", b=128)[:, 0:8, :]

                def tr(e, na=na, pTv=pTv):
                    for c in range(8):
                        i = e.transpose(out=pTv[:, c, :], in_=na.ap[:, c * 128:(c + 1) * 128], identity=ident.ap)
                    return i
                P.op("pe", tr, reads=[na.r, ident.r], writes=[psT.r])
                P.op("dve", lambda e, pi=pi, pTv=pTv: e.tensor_tensor(out=mixT.ap[:, 4:12, pi * 128:(pi + 1) * 128], in0=pTv,
                                                             in1=gB.ap[:, :, pi * 128:(pi + 1) * 128], op=ALU.mult),
                     reads=[psT.r, gB.r], writes=[bf["mixB"]])
                yield

        def pool_mix(seg, tok0, T, bf):
            mixT, gts = bf["mixT"], bf["gts"]
            halo_load(ubuf, seg.uT, tok0, T, 8, seg.ntok)
            halo_load(mbuf, seg.mask, tok0, T, 8, seg.ntok, is_mask=True)
            W = T + 16
            m2, m4, m8, tt = tmpA
            P.op("dve", lambda e: e.tensor_tensor(out=m2.ap[:, 0:W - 1], in0=mbuf.ap[:, 0:W - 1], in1=mbuf.ap[:, 1:W], op=ALU.add), reads=[mbuf.r], writes=[m2.r])
            P.op("dve", lambda e: e.tensor_tensor(out=m4.ap[:, 0:W - 3], in0=m2.ap[:, 0:W - 3], in1=m2.ap[:, 2:W - 1], op=ALU.add), reads=[m2.r], writes=[m4.r])
            P.op("dve", lambda e: e.tensor_tensor(out=m8.ap[:, 0:W - 7], in0=m4.ap[:, 0:W - 7], in1=m4.ap[:, 4:W - 3], op=ALU.add), reads=[m4.r], writes=[m8.r])
            srcs = [(mbuf, 1), (m2, 2), (m4, 4), (m8, 8)]
            for g, (sb, sh) in enumerate(srcs):
                P.op("dve", lambda e, g=g, sb=sb, sh=sh: e.tensor_tensor(out=icn.ap[:, g, 0:T], in0=sb.ap[:, 8 - sh:8 - sh + T], in1=sb.ap[:, 8:8 + T], op=ALU.add),
                     reads=[sb.r], writes=[icn.r])
            P.op("dve", lambda e: e.tensor_scalar(out=icn.ap[:, :, 0:T], in0=icn.ap[:, :, 0:T], scalar1=1.0, scalar2=None, op0=ALU.max), reads=[icn.r], writes=[icn.r])
            P.op("dve", lambda e: e.reciprocal(out=icn.ap[:, :, 0:T], in_=icn.ap[:, :, 0:T]), reads=[icn.r], writes=[icn.r])
            yield
            for g in range(4):
                u = ubuf.ap[:, g, :]
                a2, a4, a8, s_ = tmpA
                if g >= 1:
                    P.op("dve", lambda e, u=u: e.tensor_tensor(out=a2.ap[:, 0:W - 1], in0=u[:, 0:W - 1], in1=u[:, 1:W], op=ALU.add), reads=[ubuf.r], writes=[a2.r])
                if g >= 2:
                    P.op("dve", lambda e: e.tensor_tensor(out=a4.ap[:, 0:W - 3], in0=a2.ap[:, 0:W - 3], in1=a2.ap[:, 2:W - 1], op=ALU.add), reads=[a2.r], writes=[a4.r])
                if g >= 3:
                    P.op("dve", lambda e: e.tensor_tensor(out=a8.ap[:, 0:W - 7], in0=a4.ap[:, 0:W - 7], in1=a4.ap[:, 4:W - 3], op=ALU.add), reads=[a4.r], writes=[a8.r])
                sh = [1, 2, 4, 8][g]
                srcT = [ubuf, a2, a4, a8][g]
                sap = u if g == 0 else srcT.ap
                P.op("dve", lambda e, sap=sap, sh=sh: e.tensor_tensor(out=s_.ap[:, 0:T], in0=sap[:, 8 - sh:8 - sh + T], in1=sap[:, 8:8 + T], op=ALU.add),
                     reads=[srcT.r], writes=[s_.r])
                P.op("dve", lambda e, g=g: e.tensor_tensor(out=s_.ap[:, 0:T], in0=s_.ap[:, 0:T], in1=icn.ap[:, g, 0:T], op=ALU.mult), reads=[s_.r, icn.r], writes=[s_.r])
                P.op("dve", lambda e, g=g, u=u: e.tensor_tensor(out=pooled.ap[:, g, 0:T], in0=s_.ap[:, 0:T], in1=u[:, 8:8 + T], op=ALU.subtract),
                     reads=[s_.r, ubuf.r], writes=[pooled.r])
                ps = psG.next()
                P.op("pe", lambda e, g=g, ps=ps: e.matmul(ps.ap[:, 0:T], lhsT=wpool_s.ap[:, g, :], rhs=pooled.ap[:, g, 0:T], start=True, stop=True),
                     reads=[wpool_s.r, pooled.r], writes=[ps.r])
                P.op("dve", lambda e, g=g, ps=ps: e.scalar_tensor_tensor(out=mixT.ap[:, g, 0:T], in0=ps.ap[:, 0:T], scalar=pscale_s.ap[:, l, g:g + 1],
                                                                         in1=gts.ap[:, g, 0:T], op0=ALU.mult, op1=ALU.mult),
                     reads=[ps.r, pscale_s.r, gts.r], writes=[bf["mixA"]])
                yield

        def conv_mix(seg, tok0, T, bf):
            mixT, gts = bf["mixT"], bf["gts"]
            halo_load(ybuf, seg.yT, tok0, T, 15, seg.ntok)
            for c in range(4):
                acc = cacc.ap[:, c, 0:T]
                P.op("dve", lambda e, c=c, acc=acc: e.tensor_scalar(out=acc, in0=ybuf.ap[:, c, 0:T], scalar1=cdw_s.ap[:, l, c, 0:1],
                                                                    scalar2=cvecs_s.ap[:, l, c, 0:1], op0=ALU.mult, op1=ALU.add),
                     reads=[ybuf.r, cdw_s.r, cvecs_s.r], writes=[caccr[c]])
            yield
            for j in range(1, 31):
                for c in range(4):
                    acc = cacc.ap[:, c, 0:T]
                    P.op("dve", lambda e, c=c, acc=acc, j=j: e.scalar_tensor_tensor(out=acc, in0=ybuf.ap[:, c, j:j + T], scalar=cdw_s.ap[:, l, c, j:j + 1],
                                                                                   in1=acc, op0=ALU.mult, op1=ALU.add),
                         reads=[ybuf.r, cdw_s.r, caccr[c]], writes=[caccr[c]])
                yield
            psm = psG.next()
            psq = psG.next()

            def st1(e):
                for c in range(4):
                    i = e.matmul(psm.ap[:, 0:T], lhsT=onesm.ap, rhs=cacc.ap[:, c, 0:T], start=(c == 0), stop=(c == 3))
                return i
            P.op("pe", st1, reads=[onesm.r] + caccr, writes=[psm.r])
            for c in range(4):
                P.op("act", lambda e, c=c: e.activation(out=csq.ap[:, 0:T], in_=cacc.ap[:, c, 0:T], func=AF.Square), reads=[caccr[c]], writes=[csq.r])
                P.op("pe", lambda e, c=c: e.matmul(psq.ap[:, 0:T], lhsT=onesm.ap, rhs=csq.ap[:, 0:T], start=(c == 0), stop=(c == 3)),
                     reads=[onesm.r, csq.r], writes=[psq.r], chan=None)
            yield
            P.op("act", lambda e: e.activation(out=cmean.ap[:, 0:T], in_=psm.ap[:, 0:T], func=AF.Copy), reads=[psm.r], writes=[cmean.r])
            P.op("dve", lambda e: e.tensor_tensor(out=cvar.ap[:, 0:T], in0=cmean.ap[:, 0:T], in1=cmean.ap[:, 0:T], op=ALU.mult), reads=[cmean.r], writes=[cvar.r])
            P.op("dve", lambda e: e.tensor_tensor(out=cvar.ap[:, 0:T], in0=psq.ap[:, 0:T], in1=cvar.ap[:, 0:T], op=ALU.subtract), reads=[psq.r, cvar.r], writes=[cvar.r])
            P.op("act", lambda e: e.activation(out=cvar.ap[:, 0:T], in_=cvar.ap[:, 0:T], func=AF.Ln, bias=eps5.ap[:, 0:1]), reads=[cvar.r, eps5.r], writes=[cvar.r])
            P.op("act", lambda e: e.activation(out=cvar.ap[:, 0:T], in_=cvar.ap[:, 0:T], func=AF.Exp, scale=-0.5), reads=[cvar.r], writes=[cvar.r])
            yield
            for c in range(4):
                P.op("dve", lambda e, c=c: e.tensor_tensor(out=cd.ap[:, 0:T], in0=cacc.ap[:, c, 0:T], in1=cmean.ap[:, 0:T], op=ALU.subtract),
                     reads=[caccr[c], cmean.r], writes=[cd.r])
                P.op("dve", lambda e: e.tensor_tensor(out=cd.ap[:, 0:T], in0=cd.ap[:, 0:T], in1=cvar.ap[:, 0:T], op=ALU.mult), reads=[cd.r, cvar.r], writes=[cd.r])
                P.op("act", lambda e, c=c: e.activation(out=cz.ap[:, c, 0:T], in_=cd.ap[:, 0:T], func=AF.Silu, scale=cvecs_s.ap[:, l, c, 1:2], bias=cvecs_s.ap[:, l, c, 2:3]),
                     reads=[cd.r, cvecs_s.r], writes=[cz.r])
                yield
            for dch in range(4):
                ps = psG.next()

                def pw(e, ps=ps, dch=dch):
                    for c in range(4):
                        i = e.matmul(ps.ap[:, 0:T], lhsT=cpw_s.ap[:, c, dch * 128:(dch + 1) * 128], rhs=cz.ap[:, c, 0:T], start=(c == 0), stop=(c == 3))
                    return i
                P.op("pe", pw, reads=[cpw_s.r, cz.r], writes=[ps.r])
                P.op("dve", lambda e, ps=ps, dch=dch: e.scalar_tensor_tensor(out=mixT.ap[:, 12 + dch, 0:T], in0=ps.ap[:, 0:T], scalar=cvecs_s.ap[:, l, dch, 3:4],
                                                                             in1=gts.ap[:, 4 + dch, 0:T], op0=ALU.add, op1=ALU.mult),
                     reads=[ps.r, cvecs_s.r, gts.r], writes=[bf["mixC"]])
                yield

        def out_proj(seg, tok0, T, bf, final, out_tok0):
            mixT = bf["mixT"]
            nt = T // 128
            xs = seg.x[l]
            xns = [xnew.next() for _ in range(nt)]
            for cg in range(4):
                wo = wos.next()
                dma_in(wo, wo.ap, woutbf[l][:, :, cg * 512:(cg + 1) * 512].rearrange("f p c -> p f c"), [wres["wout%d" % l]])
                for ti in range(nt):
                    xn_ = xns[ti]
                    t0 = tok0 + ti * 128
                    xi = xin.next()
                    dma_in(xi, xi.ap, xs.ap[t0:t0 + 128, cg * 512:(cg + 1) * 512], xs.res(t0, t0 + 128))
                    ps = psG.next()

                    def mm(e, ps=ps, ti=ti, wo=wo):
                        for fc in range(16):
                            i = e.matmul(ps.ap, lhsT=mixT.ap[:, fc, ti * 128:(ti + 1) * 128], rhs=wo.ap[:, fc, :],
                                         start=(fc == 0), stop=(fc == 15))
                        return i
                    P.op("pe", mm, reads=[bf["mixA"], bf["mixB"], bf["mixC"], wo.r], writes=[ps.r])
                    dst = xn_.ap[:, cg * 512:(cg + 1) * 512]
                    P.op("dve", lambda e, ps=ps, dst=dst, cg=cg: e.tensor_tensor(out=dst, in0=ps.ap, in1=gbc.ap[:, cg * 512:(cg + 1) * 512], op=ALU.mult),
                         reads=[ps.r, gbc.r], writes=[xn_.r])
                    P.op("pool", lambda e, dst=dst, xi=xi: e.tensor_tensor(out=dst, in0=dst, in1=xi.ap, op=ALU.add), reads=[xn_.r, xi.r], writes=[xn_.r])
                    yield
            for ti in range(nt):
                xn_ = xns[ti]
                t0 = tok0 + ti * 128
                if not final:
                    x1 = seg.x[1]
                    dma_out(xn_, xn_.ap, x1.ap[t0:t0 + 128, :], x1.res(t0, t0 + 128))
                else:
                    ss = ssC.next()
                    P.op("pool", lambda e, ss=ss: e.memset(ss.ap, 0.0), writes=[ss.r])
                    P.op("act", lambda e, ss=ss, xn_=xn_: e.activation(out=junkC.ap, in_=xn_.ap, func=AF.Square, accum_out=ss.ap[:, 0:1]),
                         reads=[xn_.r, ss.r], writes=[ss.r, junkC.r])
                    P.op("act", lambda e, ss=ss: e.activation(out=ss.ap[:, 1:2], in_=ss.ap[:, 0:1], func=AF.Ln, scale=1.0 / D, bias=eps6.ap[:, 0:1]),
                         reads=[ss.r, eps6.r], writes=[ss.r])
                    P.op("act", lambda e, ss=ss: e.activation(out=ss.ap[:, 2:3], in_=ss.ap[:, 1:2], func=AF.Exp, scale=-0.5), reads=[ss.r], writes=[ss.r])
                    P.op("dve", lambda e, ss=ss, xn_=xn_: e.scalar_tensor_tensor(out=xn_.ap, in0=xn_.ap, scalar=ss.ap[:, 2:3], in1=fbc.ap, op0=ALU.mult, op1=ALU.mult),
                         reads=[xn_.r, ss.r, fbc.r], writes=[xn_.r])
                    o0 = out_tok0 + ti * 128
                    dma_out(xn_, xn_.ap, out[o0:o0 + 128, :], outres.res(o0, o0 + 128))
                yield

        wos = Rot([A([16, 512], BF16, "wo") for _ in range(2)])
        gstate = {"m": -1}

        def mixM(blk, bf):
            seg, tok0, T, final, out_tok0 = blk
            P.op("sp", lambda e: e.dma_start(out=gtsAC.ap[:, 0:4, 0:T], in_=seg.gate.ap[0:4, :, tok0:tok0 + T].rearrange("c p t -> p c t")),
                 reads=seg.gate.res(tok0, tok0 + T), writes=[gtsAC.r], chan=gtsAC.chan)
            P.op("sp", lambda e: e.dma_start(out=gtsAC.ap[:, 4:8, 0:T], in_=seg.gate.ap[12:16, :, tok0:tok0 + T].rearrange("c p t -> p c t")),
                 reads=seg.gate.res(tok0, tok0 + T), writes=[gtsAC.r], chan=gtsAC.chan)
            yield from pool_mix(seg, tok0, T, bf)
            yield from conv_mix(seg, tok0, T, bf)

        def mixO(blk, bf):
            seg, tok0, T, final, out_tok0 = blk
            m = seg.midx
            if gstate["m"] != m:
                P.op("sp", lambda e: e.dma_start(out=gbc.ap, in_=modrow[2 * l + m:2 * l + m + 1, :].broadcast_to([128, D])),
                     reads=[wres["modrow"]], writes=[gbc.r], chan=gbc.chan)
                gstate["m"] = m
            yield from out_proj(seg, tok0, T, bf, final, out_tok0)

        blocks = [(lat, r * GW, 4 * GW, l == 1, (r - 8) * GW) for r in range(r_lo, r_hi, 4)]
        if l == 0:
            blocks.append((cseg, 0, CTX, False, 0))
        def cast_next_layer():
            cf = A([1664], F32, "cf")
            cb = A([1664], BF16, "cb")
            for kc in range(16):
                for part in range(4):
                    sl = slice(part * 1664, (part + 1) * 1664)
                    dma_in(cf, cf.ap, w_in[1, kc * 128:(kc + 1) * 128, sl], [])
                    P.op("pool", lambda e: e.tensor_copy(out=cb.ap, in_=cf.ap), reads=[cf.r], writes=[cb.r])
                    P.op("pool", lambda e, kc=kc, sl=sl: e.dma_start(out=wbf[1][kc * 128:(kc + 1) * 128, sl], in_=cb.ap),
                         reads=[cb.r], writes=[wres["wbf1"]], chan=cb.chan + "o")
                    yield
        auxg2 = cast_next_layer() if l == 0 else iter(())
        auxn = {"n": 0}
        def step(g):
            try:
                next(g)
                return False
            except StopIteration:
                return True

        attn_loads(blocks[0][0], blocks[0][1], blocks[0][2], bufs[0])
        ga = attention(blocks[0][0], blocks[0][1], blocks[0][2], bufs[0])
        gm = mixM(blocks[0], bufs[0])
        da = dm = False
        first = True
        while not (da and dm):
            if not da:
                da = step(ga)
                if da and len(blocks) > 1:
                    attn_loads(blocks[1][0], blocks[1][1], blocks[1][2], bufs[1])
            if not dm:
                dm = step(gm)
        for bi, blk in enumerate(blocks):
            go = mixO(blk, bufs[bi % 2])
            if bi + 1 < len(blocks):
                nb = blocks[bi + 1]
                ga = attention(nb[0], nb[1], nb[2], bufs[(bi + 1) % 2])
                gm = mixM(nb, bufs[(bi + 1) % 2])
                da = dm = False
            else:
                da = dm = True
            do = False
            it = 0
            while not (da and dm and do):
                it += 1
                auxn["n"] += 1
                if auxn["n"] % 8 == 0:
                    next(auxg2, None)
                if not da:
                    da = step(ga)
                    if da and bi + 2 < len(blocks):
                        nb2 = blocks[bi + 2]
                        attn_loads(nb2[0], nb2[1], nb2[2], bufs[bi % 2])
                if not dm:
                    dm = step(gm)
                if not do and (it % 4 == 0 or (da and dm)):
                    do = step(go)
        for _ in auxg2:
            pass
        chk(4 + 10 * l)

    try:
        chk(0)
        layer(0)
        layer(1)
    except _Stop:
        pass
    P.op("sp", lambda e: e.nop(), reads=outres.rs)
    P.emit()
    return nc


def _fm(v):
    v = np.asarray(v, np.float32)
    k = v.shape[-1] // 128
    return np.ascontiguousarray(np.moveaxis(v.reshape(v.shape[:-1] + (k, 128)), -1, 0))


def _eb_tables(rpb, OWN, GR, j):
    out = np.full((5, 128, H, NSLOT, 128), NEG, np.float32)
    qc = np.arange(64)
    cs = np.clip(qc - 8, 0, 48)
    kc = np.arange(64)
    colok = (kc[:, None] >= cs[None, :]) & (kc[:, None] < cs[None, :] + 16)
    dc = np.clip(kc[:, None] - qc[None, :] + 15, 0, 30)
    own_lo, own_hi = 4, 4 + OWN // 2
    pairs = {0: None, 1: own_lo, 2: own_lo + 1, 3: own_hi - 2, 4: own_hi - 1}
    for cls in range(5):
        rel = CLS_SLOTS[cls]
        for s, r in enumerate(rel):
            for kr2 in range(2):
                for qr2 in range(2):
                    if cls == 0:
                        dr = (2 * r + kr2) - qr2
                        ok = -4 <= dr <= 3
                    else:
                        n = pairs[cls]
                        gq = OWN * j - 8 + 2 * n + qr2
                        gk = OWN * j - 8 + 2 * (n + r) + kr2
                        rs = min(max(gq - 4, 0), GR - 8)
                        ok = rs <= gk < rs + 8
                        dr = gk - gq
                    if not ok:
                        continue
                    blk = np.where(colok[None], rpb[:, dr + 7, :][:, dc], NEG)
                    out[cls, kr2 * 64:(kr2 + 1) * 64, :, s, qr2 * 64:(qr2 + 1) * 64] = np.transpose(blk, (1, 0, 2))
    return out.reshape(5, 128, H * NSLOT * 128)


def prep(inp, OWN, cores):
    x = np.asarray(inp["x"], np.float32)
    B, L, _ = x.shape
    GR = L // GW
    NJ = GR // OWN
    E = OWN + 16
    NTOK = E * GW
    p = np.arange(128)
    d = p % 64
    half = d // 32
    i32 = d % 32
    fi = i32 % 16
    freqs = (10000.0 ** (-np.arange(0, 32, 2, dtype=np.float32) / 32)).astype(np.float32)
    perm = np.zeros((128, 128), np.float32)
    partner = np.where(p % 32 < 16, p + 16, p - 16)
    perm[partner, p] = 1.0
    sign = np.where(i32 < 16, -1.0, 1.0).astype(np.float32)
    shared = {
        "w_mod": np.ascontiguousarray(inp["w_mod"], np.float32),
        "bmod": _fm(inp["b_mod"]).reshape(128, 96),
        "normg": _fm(inp["norm_g"]).reshape(128, 32),
        "w_in": np.ascontiguousarray(inp["w_in"], np.float32),
        "w_pool": np.ascontiguousarray(inp["w_pool"], np.float32),
        "pscale": _fm(inp["pool_scale"]).reshape(128, 8),
        "cdw": np.ascontiguousarray(np.transpose(_fm(inp["conv_dw"]), (0, 1, 3, 2))).reshape(128, 2 * 4 * 31),
        "cvecs": np.ascontiguousarray(np.stack([_fm(inp["conv_dw_b"]), _fm(inp["conv_ln_g"]), _fm(inp["conv_ln_b"]), _fm(inp["conv_pw_b"])], -1)).reshape(128, 32),
        "conv_pw": np.ascontiguousarray(inp["conv_pw"], np.float32),
        "w_out": np.ascontiguousarray(inp["w_out"], np.float32),
        "fng": np.asarray(inp["final_norm_g"], np.float32).reshape(1, D),
        "cmask": np.ones((128, CTX), np.float32),
        "permin": perm,
    }
    maps = []
    for (b, j) in cores:
        g = OWN * j - 8 + np.arange(E)
        valid = (g >= 0) & (g < GR)
        xl = np.zeros((E, GW, D), np.float32)
        xl[valid] = x[b].reshape(GR, GW, D)[g[valid]]
        rows = np.repeat(g, GW).astype(np.float32)
        cols = np.tile(np.arange(GW), E).astype(np.float32)
        pos = np.where(half[:, None] == 0, rows[None, :], cols[None, :]).astype(np.float32)
        ang = (pos * freqs[fi][:, None]).astype(np.float32)
        m = dict(shared)
        m["xloc"] = xl.reshape(NTOK, D)
        m["ctxin"] = np.ascontiguousarray(inp["ctx"][b], np.float32)
        m["cvec"] = np.ascontiguousarray(np.stack([_fm(inp["c"][b]), _fm(inp["c_ctx"])], -1)).reshape(128, 32)
        m["cosT"] = np.cos(ang).astype(np.float32)
        m["sinT"] = (np.sin(ang) * sign[:, None]).astype(np.float32)
        m["maskT"] = np.ascontiguousarray(np.broadcast_to(np.repeat(valid, GW).astype(np.float32)[None, :], (128, NTOK)))
        m["ebraw"] = np.stack([_eb_tables(np.asarray(inp["na_rpb"][l], np.float32), OWN, GR, j) for l in range(2)])
        maps.append(m)
    return maps


_NC_CACHE = {}


def kernel(**inputs):
    OWN = 64
    x = np.asarray(inputs["x"])
    B, L, _ = x.shape
    NJ = (L // GW) // OWN
    cores = [(b, j) for b in range(B) for j in range(NJ)]
    maps = prep(inputs, OWN, cores)
    if OWN not in _NC_CACHE:
        _NC_CACHE[OWN] = build(OWN)
    nc = _NC_CACHE[OWN]
    res = run_bass_kernel_spmd(nc, maps, core_ids=list(range(len(cores))))
    out = np.zeros((B, L, D), np.float32)
    for i, (b, j) in enumerate(cores):
        out[b, j * OWN * GW:(j + 1) * OWN * GW] = res.results[i]["out"]
    return out
```
